# Optimizing a Trainium2 kernel written in Bass

```python
import math
import jax, jax.numpy as jnp
from jax import lax
import numpy as np

D_MODEL = 1024
BATCH = 8
SEQ = 2048
DEPTH = 1

EXPAND = 2
D_MIX = EXPAND * D_MODEL
GROUP_W = D_MIX // 2
RET_HEADS = 8
RET_HEAD_DIM = GROUP_W // RET_HEADS
CHUNK = 128
ROPE_THETA = 10000.0
POOL_WINDOWS = (2, 4, 8, 16)
N_POOL = len(POOL_WINDOWS)
POOL_C = GROUP_W // N_POOL
N_PROJ = 6
EPS = 1e-6

kernel_name = "hybrid_retention_pool_parallel_heads"


def rms_norm(x, g):
    xf = x.astype(jnp.float32)
    y = xf * lax.rsqrt(jnp.mean(xf * xf, axis=-1, keepdims=True) + EPS)
    return (y * g.astype(jnp.float32)).astype(x.dtype)


def apply_rope(t, positions):
    half = t.shape[-1] // 2
    inv_freq = ROPE_THETA ** (-jnp.arange(half, dtype=jnp.float32) / half)
    ang = positions.astype(jnp.float32)[:, :, None] * inv_freq[None, None, :]
    cos = jnp.cos(ang)[:, :, None, :]
    sin = jnp.sin(ang)[:, :, None, :]
    tf = t.astype(jnp.float32)
    t1, t2 = tf[..., :half], tf[..., half:]
    return jnp.concatenate([t1 * cos - t2 * sin, t1 * sin + t2 * cos], axis=-1)


def retention_chunkwise(q, k, v):
    b, s, h, d = q.shape
    n = s // CHUNK
    def to_chunks(t):
        return t.reshape(b, n, CHUNK, h, d).transpose(0, 3, 1, 2, 4)
    qc, kc, vc = to_chunks(q), to_chunks(k), to_chunks(v)
    gammas = 1.0 - 2.0 ** (-5.0 - jnp.arange(h, dtype=jnp.float32))
    log_g = jnp.log(gammas)
    idx = jnp.arange(CHUNK, dtype=jnp.float32)
    rel = idx[:, None] - idx[None, :]
    decay = jnp.where(rel[None] >= 0,
                      jnp.exp(log_g[:, None, None] * jnp.maximum(rel, 0.0)[None]),
                      0.0)
    scores = jnp.einsum('bhncd,bhnsd->bhncs', qc, kc) * decay[None, :, None]
    inner = jnp.einsum('bhncs,bhnse->bhnce', scores, vc)
    zeta = jnp.exp(log_g[:, None] * (CHUNK - 1.0 - idx)[None, :])
    kv = jnp.einsum('bhnsd,hs,bhnse->bhnde', kc, zeta, vc)
    chunk_decay = jnp.exp(log_g * CHUNK)[None, :, None, None]

    def step(state, kv_n):
        return chunk_decay * state + kv_n, state

    init = jnp.zeros((b, h, d, d), jnp.float32)
    _, r_prev = lax.scan(step, init, jnp.moveaxis(kv, 2, 0))
    r_prev = jnp.moveaxis(r_prev, 0, 2)
    xi = jnp.exp(log_g[:, None] * (idx + 1.0)[None, :])
    cross = jnp.einsum('bhncd,bhnde->bhnce', qc, r_prev) * xi[None, :, None, :, None]
    out = inner + cross
    return out.transpose(0, 2, 3, 1, 4).reshape(b, s, h, d)


def retention_branch(q, k, v, positions, norm_g):
    b, s, _ = q.shape
    shp = (b, s, RET_HEADS, RET_HEAD_DIM)
    qh = apply_rope(q.reshape(shp), positions)
    kh = apply_rope(k.reshape(shp), positions) * (RET_HEAD_DIM ** -0.5)
    vh = v.reshape(shp).astype(jnp.float32)
    o = retention_chunkwise(qh, kh, vh)
    mu = jnp.mean(o, axis=-1, keepdims=True)
    var = jnp.mean(jnp.square(o - mu), axis=-1, keepdims=True)
    o = ((o - mu) * lax.rsqrt(var + EPS)).reshape(b, s, GROUP_W)
    return o * norm_g.astype(jnp.float32)


def pooling_branch(u, pool_w, pool_scale):
    b, s, _ = u.shape
    uf = u.astype(jnp.float32)
    cs = jnp.cumsum(uf, axis=1)
    t = jnp.arange(s, dtype=jnp.float32)
    outs = []
    for g, w in enumerate(POOL_WINDOWS):
        sl = slice(g * POOL_C, (g + 1) * POOL_C)
        cg = cs[..., sl]
        shifted = jnp.pad(cg, ((0, 0), (w, 0), (0, 0)))[:, :s]
        count = jnp.minimum(t + 1.0, float(w))[None, :, None]
        mixed = (cg - shifted) / count - uf[..., sl]
        outs.append(jnp.einsum('bsc,cd->bsd', mixed, pool_w[g].astype(jnp.float32)))
    return jnp.concatenate(outs, axis=-1) * pool_scale.astype(jnp.float32)


def hybrid_layer(x, positions, w_in, w_out, pool_w, pool_scale, ret_norm_g, pre_g, post_g):
    h = rms_norm(x, pre_g)
    proj = jnp.einsum('bsd,df->bsf', h, w_in)
    q, k, v, g_ret, u, g_pool = jnp.split(proj, N_PROJ, axis=-1)
    y_ret = retention_branch(q, k, v, positions, ret_norm_g) * jax.nn.silu(g_ret.astype(jnp.float32))
    y_pool = pooling_branch(u, pool_w, pool_scale) * jax.nn.silu(g_pool.astype(jnp.float32))
    y = jnp.concatenate([y_ret, y_pool], axis=-1).astype(x.dtype)
    y = jnp.einsum('bsf,fd->bsd', y, w_out)
    return x + rms_norm(y, post_g)


def setup_inputs(seed: int = 0) -> dict:
    key = jax.random.key(seed)
    ks = jax.random.split(key, 10)
    x = jax.random.normal(ks[0], (BATCH, SEQ, D_MODEL), jnp.float32)
    offs = jax.random.randint(ks[1], (BATCH, 1), 0, 1024, dtype=jnp.int32)
    positions = (offs + jnp.arange(SEQ, dtype=jnp.int32)[None, :]).astype(jnp.int32)
    w_in = jax.random.normal(ks[2], (DEPTH, D_MODEL, N_PROJ * GROUP_W), jnp.float32) * D_MODEL ** -0.5
    w_out = jax.random.normal(ks[3], (DEPTH, D_MIX, D_MODEL), jnp.float32) * D_MIX ** -0.5
    pool_w = jax.random.normal(ks[4], (DEPTH, N_POOL, POOL_C, POOL_C), jnp.float32) * POOL_C ** -0.5
    pool_scale = 1.0 + 0.1 * jax.random.normal(ks[5], (DEPTH, GROUP_W), jnp.float32)
    ret_norm_g = 1.0 + 0.02 * jax.random.normal(ks[6], (DEPTH, GROUP_W), jnp.float32)
    pre_norm_g = 1.0 + 0.02 * jax.random.normal(ks[7], (DEPTH, D_MODEL), jnp.float32)
    post_norm_g = 1.0 + 0.02 * jax.random.normal(ks[8], (DEPTH, D_MODEL), jnp.float32)
    return {"x": x, "positions": positions, "w_in": w_in, "w_out": w_out,
            "pool_w": pool_w, "pool_scale": pool_scale, "ret_norm_g": ret_norm_g,
            "pre_norm_g": pre_norm_g, "post_norm_g": post_norm_g}


def reference(x, positions, w_in, w_out, pool_w, pool_scale, ret_norm_g, pre_norm_g, post_norm_g):
    for layer in range(DEPTH):
        x = hybrid_layer(x, positions, w_in[layer], w_out[layer], pool_w[layer],
                         pool_scale[layer], ret_norm_g[layer], pre_norm_g[layer],
                         post_norm_g[layer])
    return x
```

```python
import math
from contextlib import ExitStack

import numpy as np
import concourse.bass as bass
import concourse.mybir as mybir
from concourse.bass_utils import run_bass_kernel_spmd

F32 = mybir.dt.float32
BF16 = mybir.dt.bfloat16
I32 = mybir.dt.int32
AF = mybir.ActivationFunctionType
ALU = mybir.AluOpType

D = 1024
S = 2048
H = 8
NT = 16
NTB = 4
EPS = 1e-6
WINDOWS = (2, 4, 8, 16)
TWO_PI = 2.0 * math.pi
CW1 = 6.28125
CW2 = TWO_PI - CW1
PI_SAFE = 3.1415925


class _Op:
    __slots__ = ("eng", "fn", "deps", "is_dma", "dma_sem", "dma_val", "signal", "count", "name", "idx", "waits", "know")


class Sched:
    ENGS = ("pe", "act", "dve", "pool", "sp")

    def __init__(self, nc):
        self.nc = nc
        self.ops = []
        self.last_writer = {}
        self.readers = {}
        self.dma_sem_count = {}
        self.barrier_deps = []
        self.last_of_eng = {}
        self.last_of_dsem = {}
        self.excl_last = {}

    def add(self, eng, fn, reads=(), writes=(), dma_sem=None, name="", after=()):
        op = _Op()
        op.eng = eng; op.fn = fn; op.name = name
        op.is_dma = dma_sem is not None
        op.dma_sem = dma_sem
        op.signal = False; op.count = 0; op.dma_val = 0
        op.idx = len(self.ops)
        deps = set(self.barrier_deps)
        deps.update(after)
        for r in reads:
            w = self.last_writer.get(r)
            if w is not None:
                deps.add(w)
        for w_ in writes:
            lw = self.last_writer.get(w_)
            if lw is not None:
                deps.add(lw)
            for rd in self.readers.get(w_, ()):
                deps.add(rd)
        banks = set()
        for r in list(reads) + list(writes):
            if isinstance(r, tuple) and r[0] == "ps":
                banks.add(r[1])
        for b in banks:
            g = self.excl_last.get(b)
            if g is not None and g[0] != eng:
                deps.add(g[1])
            self.excl_last[b] = (eng, op)
        deps.discard(op)
        op.deps = deps
        for r in reads:
            self.readers.setdefault(r, []).append(op)
        for w_ in writes:
            self.last_writer[w_] = op
            self.readers[w_] = []
        if op.is_dma:
            c = self.dma_sem_count.get(id(dma_sem), 0) + 16
            self.dma_sem_count[id(dma_sem)] = c
            op.dma_val = c
            self.last_of_dsem[id(dma_sem)] = op
        else:
            self.last_of_eng[eng] = op
        self.ops.append(op)
        return op

    def mark(self):
        return list(self.last_of_eng.values()) + list(self.last_of_dsem.values())

    def barrier(self):
        self.barrier_deps = list(self.last_of_eng.values()) + list(self.last_of_dsem.values())

    def emit(self, sems, final_waits=()):
        nc = self.nc
        for op in self.ops:
            for d in op.deps:
                if d.is_dma:
                    continue
                if d.eng == "pe" and op.eng == "pe" and not op.is_dma:
                    continue
                d.signal = True
        counts = {e: 0 for e in self.ENGS}
        for op in self.ops:
            if op.is_dma:
                continue
            if op.signal:
                counts[op.eng] += 1
                op.count = counts[op.eng]
        per_eng = {e: [o for o in self.ops if o.eng == e] for e in self.ENGS}

        eng_know = {e: {} for e in self.ENGS}
        self.n_waits = 0
        self.n_skipped = 0
        for op in self.ops:
            need = {}
            for d in op.deps:
                if d.is_dma:
                    key = ("dma", id(d.dma_sem)); sem = d.dma_sem; val = d.dma_val
                else:
                    if d.eng == "pe" and op.eng == "pe" and not op.is_dma:
                        continue
                    key = d.eng; sem = sems[d.eng]; val = d.count
                if val > need.get(key, (None, 0, None))[1]:
                    need[key] = (sem, val, d)
            know = eng_know[op.eng]
            waits = []
            for key, (sem, val, d) in sorted(need.items(), key=lambda kv: -kv[1][1]):
                if know.get(key, 0) >= val:
                    self.n_skipped += 1
                    continue
                waits.append((sem, val))
                self.n_waits += 1
                know[key] = val
                for k2, v2 in d.know.items():
                    if v2 > know.get(k2, 0):
                        know[k2] = v2
            op.waits = waits
            op.know = dict(know)
            if op.is_dma:
                op.know[("dma", id(op.dma_sem))] = op.dma_val
            elif op.signal:
                op.know[op.eng] = max(op.know.get(op.eng, 0), op.count)

        def run_stream(e, engobj):
            for op in per_eng[e]:
                for sem, val in op.waits:
                    engobj.wait_ge(sem, val)
                ins = op.fn(engobj)
                if op.is_dma:
                    ins.then_inc(op.dma_sem, 16)
                elif op.signal:
                    ins.then_inc(sems[op.eng], 1)
            if e == "sp":
                for sem, val in final_waits:
                    engobj.wait_ge(sem, val)

        with nc.Block() as block:
            @block.tensor
            def _(eng):
                run_stream("pe", eng)

            @block.scalar
            def _(eng):
                run_stream("act", eng)

            @block.vector
            def _(eng):
                run_stream("dve", eng)

            @block.gpsimd
            def _(eng):
                run_stream("pool", eng)

            @block.sync
            def _(eng):
                run_stream("sp", eng)


def bc(ap2d, n_mid):
    (ps, pn), (st, n) = ap2d.ap
    return bass.AP(ap2d.tensor, ap2d.offset, [[ps, pn], [0, n_mid], [st, n]])


def col_bc(ap2d, n_in):
    (ps, pn), (st, k) = ap2d.ap
    return bass.AP(ap2d.tensor, ap2d.offset, [[ps, pn], [st, k], [0, n_in]])


def build_program():
    nc = bass.Bass("TRN2", target_bir_lowering=False)
    x = nc.dram_tensor("x", [S, D], F32, kind="ExternalInput")
    pos = nc.dram_tensor("pos", [1, S], I32, kind="ExternalInput")
    w_in = nc.dram_tensor("w_in", [D, 6 * D], F32, kind="ExternalInput")
    w_out = nc.dram_tensor("w_out", [2 * D, D], F32, kind="ExternalInput")
    pool_w = nc.dram_tensor("pool_w", [4 * 256, 256], F32, kind="ExternalInput")
    vecs = nc.dram_tensor("vecs", [128, 24], F32, kind="ExternalInput")
    gpost = nc.dram_tensor("gpost", [1, D], F32, kind="ExternalInput")
    constf = nc.dram_tensor("constf", [128, 24], F32, kind="ExternalInput")
    constb = nc.dram_tensor("constb", [128, 15 * 128], F32, kind="ExternalInput")
    out = nc.dram_tensor("out", [S, D], F32, kind="ExternalOutput")

    gam = [1.0 - 2.0 ** (-5.0 - h) for h in range(H)]
    g128 = [g ** 128 for g in gam]

    with ExitStack() as ctx:
        def sb(name, shape, dt):
            return ctx.enter_context(nc.sbuf_tensor(name, shape, dt))

        def sem(name):
            return ctx.enter_context(nc.semaphore(name))

        hT = sb("hT", [128, 8, S], BF16)
        yT = sb("yT", [128, 16, S], BF16)
        RW = sb("RW", [128, 16384], BF16)
        RP = sb("RP", [128, 18432], BF16)
        wbuf = [sb(f"wbuf{i}", [128, 8, 512], BF16) for i in range(2)]
        xst = [sb(f"xst{i}", [128, D], F32) for i in range(3)]
        xn01 = [sb(f"xn{i}", [128, D], BF16) for i in range(2)]
        cb = sb("cb", [128, 15 * 128], BF16)
        cf = sb("cf", [128, 24], F32)
        vf = sb("vf", [128, 24], F32)
        Tst = [[sb(f"Tst{i}_{k}", [128, 128], F32) for k in range(2)] for i in range(2)]
        ssq = sb("ssq", [128, 16], F32)
        rstd = sb("rstd", [128, 16], F32)
        ve_ = sb("ve", [128, 16], F32)
        mhalf = sb("mhalf", [128, 16], F32)
        halfpi = sb("halfpi", [128, 1], F32)
        actwarm = sb("actwarm", [128, 1], F32)
        bst = [sb(f"bst{i}", [128, 4, 6], F32) for i in range(2)]
        bmv = [sb(f"bmv{i}", [128, 4, 2], F32) for i in range(2)]
        gve = [sb(f"gve{i}", [128, 4], F32) for i in range(2)]
        grs = [sb(f"grs{i}", [128, 4], F32) for i in range(2)]
        gnb = [sb(f"gnb{i}", [128, 4], F32) for i in range(2)]
        fss = [sb(f"fss{i}", [128, 4], F32) for i in range(2)]
        frs = [sb(f"frs{i}", [128, 2], F32) for i in range(2)]

        cosT = RW[:, 0:4096].bitcast(F32)
        sinT = RW[:, 4096:8192].bitcast(F32)
        aq = RW[:, 8192:9216].bitcast(F32)
        bq = RW[:, 9216:10240].bitcast(F32)
        ak2 = [RW[:, 10240:11264].bitcast(F32), RW[:, 11264:12288].bitcast(F32)]
        bk = RW[:, 12288:13312].bitcast(F32)
        NQ, NK = 4, 2
        qT = [RW[:, 13312 + 512 * i:13312 + 512 * (i + 1)] for i in range(NQ)]
        kT = [RW[:, 15360 + 512 * i:15360 + 512 * (i + 1)] for i in range(NK)]
        wsb = RW[:, :].rearrange("p (a b) -> p a b", a=16)

        def rp(off, n):
            return RP[:, off:off + n]
        o = 0
        qraw = [rp(o + 512 * i, 512) for i in range(2)]; o += 1024
        kraw = [rp(o + 512 * i, 512) for i in range(2)]; o += 1024
        vTb = [rp(o + 512 * i, 512) for i in range(2)]; o += 1024
        NSG = 7
        sgr = [rp(o + 512 * i, 512) for i in range(NSG)]; o += 512 * NSG
        NV = 4
        vsb = [rp(o + 512 * i, 512).rearrange("p (a b) -> p a b", a=4) for i in range(NV)]; o += 512 * NV
        ktl = [rp(o + 512 * i, 512).rearrange("p (a b) -> p a b", a=4) for i in range(2)]; o += 1024
        SsT = [rp(o + 512 * i, 512) for i in range(2)]; o += 1024
        NR = 3
        Rbf = [rp(o + 512 * i, 512).rearrange("p (a b) -> p a b", a=4) for i in range(NR)]; o += 512 * NR
        onb = [rp(o + 512 * i, 512).rearrange("p (a b) -> p a b", a=4) for i in range(2)]; o += 1024
        assert o <= 13312, o
        sqj = RP[:, 13312:14336]
        xn = [xn01[0][:, :], xn01[1][:, :], RP[:, 14336:15360], RP[:, 15360:16384]]
        u_t = []
        for k in range(3):
            xb_ = xst[k][:, :].bitcast(BF16)
            u_t += [xb_[:, 512 * j:512 * (j + 1)] for j in range(4)]
        u_t += [RP[:, 13312 + 512 * j:13312 + 512 * (j + 1)] for j in range(4)]
        mixT = RP[:, 0:8192].rearrange("p (a b) -> p a b", a=4)
        sgp = [RP[:, 15360 + 512 * i:15360 + 512 * (i + 1)] for i in range(3)]
        ftmp = [RP[:, 2048 * i:2048 * (i + 1)].bitcast(F32) for i in range(2)]
        fout = [RP[:, 4096 + 2048 * i:4096 + 2048 * (i + 1)].bitcast(F32) for i in range(3)]
        gpb = RP[:, 10240:12288].bitcast(F32)
        NXB = 8
        xsB = [yT[:, 8 + k, :].bitcast(F32) for k in range(NXB)]
        yflat = yT[:, :, :].rearrange("p a b -> p (a b)")
        posi = yflat[:, 0:4096].bitcast(I32)
        rt = [yflat[:, 4096 + 1024 * i:4096 + 1024 * (i + 1)].bitcast(F32) for i in range(5)]
        rki = yflat[:, 4096 + 5120:4096 + 6144].bitcast(I32)

        ident = cb[:, 0:128]
        Pm = cb[:, 128:256]
        mask = cb[:, 256:384]
        def Amat(g, k):
            c0 = (3 + 3 * g + k) * 128
            return cb[:, c0:c0 + 128]
        kscale = lambda h: cf[:, h:h + 1]
        epsc = lambda h: cf[:, 8 + h:9 + h]
        invf = cf[:, 16:17]
        invf_lo = cf[:, 17:18]
        gpreT = vf[:, 0:8]
        normg = lambda h: vf[:, 8 + h:9 + h]
        pscale = lambda c: vf[:, 16 + c:17 + c]

        bank = [ctx.enter_context(nc.psum_tensor(f"bank{i}", [128, 512], F32)) for i in range(8)]
        def bkf(i):
            return bank[i][:, :]
        def bkb(i):
            return bank[i][:, :].bitcast(BF16)

        sems = {e: sem("s_" + e) for e in ("pe", "act", "dve", "pool")}
        d_c = [sem(f"d_c{i}") for i in range(5)]
        d_p = [sem(f"d_p{i}") for i in range(4)]
        d_x = [sem(f"d_x{i}") for i in range(3)]
        d_xb = [sem(f"d_xb{i}") for i in range(8)]
        d_w = [sem(f"d_w{i}") for i in range(2)]
        d_wo = sem("d_wo")
        d_pw = [sem(f"d_pw{i}") for i in range(2)]
        d_o = [[sem(f"d_o{i}_{n}") for n in range(2)] for i in range(3)]

        Sc = Sched(nc)
        PBQ, PBK, PBV, PBG, PSW, PX, PSK, PO = range(8)
        PB0, PB1, PTR, PSC, PKV, POT = PBQ, PBK, PX, PSK, PSK, PX
        A = Sc.add

        xload_ops = {}

        def x_load(t):
            if t >= NT or t in xload_ops:
                return
            xs_ = xsB[t % NXB]
            aft = [xload_ops[t - 2]] if (4 <= t < NXB) else []
            if t == 4 and "w0" in xload_ops:
                aft = aft + [xload_ops["w0"]]
            xload_ops[t] = A("sp", lambda e: e.dma_start(out=xs_, in_=x[t * 128:(t + 1) * 128, :]),
                             writes=[("xsB", t % NXB)], dma_sem=d_xb[t % NXB], after=aft)

        def posi_load(b):
            A("sp", lambda e: e.dma_start(out=posi[:, b * 512:(b + 1) * 512], in_=bass.AP(pos, b * 512, [[0, 128], [1, 512]])),
              writes=[("posi", b)] + (["ropetmp"] if b == 0 else []), dma_sem=d_p[b])
        posi_load(0)
        A("sp", lambda e: e.dma_start(out=cf[:, :], in_=constf[:, :]), writes=["cf"], dma_sem=d_c[0])
        A("sp", lambda e: e.dma_start(out=vf[:, :], in_=vecs[:, :]), writes=["vf"], dma_sem=d_c[1])
        for t in range(4):
            x_load(t)
        A("pool", lambda e: e.dma_start(out=cb[:, 0:384], in_=constb[:, 0:384]), writes=["cb"], dma_sem=d_c[3])
        WB0 = bass.AP(w_in, 0, [[6 * D, 128], [128 * 6 * D, 8], [1, 512]])
        xload_ops["w0"] = A("pool", lambda e: e.dma_start(out=wbuf[0][:, :, 0:512], in_=WB0), writes=[("wbuf", 0)],
                            dma_sem=d_w[0])
        for t in range(4, NXB):
            x_load(t)
        for b in range(1, 4):
            posi_load(b)
        A("dve", lambda e: e.memset(mhalf[:, :], -0.5), writes=["mhalf"])
        A("dve", lambda e: e.memset(halfpi[:, :], math.pi / 2), writes=["halfpi"])
        A("act", lambda e: e.activation(actwarm[:, :], mhalf[:, 0:1], AF.Silu), reads=["mhalf"], writes=["actwarm"])

        def load_wblock(slot, col0, ncols, dsem, after=()):
            src = bass.AP(w_in, col0, [[6 * D, 128], [128 * 6 * D, 8], [1, ncols]])
            A("pool", lambda e: e.dma_start(out=wbuf[slot][:, :, 0:ncols], in_=src),
              writes=[("wbuf", slot)], dma_sem=dsem, after=after)

        def pb_sq(t):
            if t >= NT:
                return
            xs_ = xsB[t % NXB]
            x_load(t)
            A("act", lambda e: e.activation(sqj[:, :], xs_, AF.Square, accum_out=ssq[:, t:t + 1]),
              reads=[("xsB", t % NXB)], writes=["sqj", ("ssq", t)])
            A("pool", lambda e: e.tensor_scalar(ve_[:, t:t + 1], ssq[:, t:t + 1], 1.0 / D, EPS, ALU.mult, ALU.add),
              reads=[("ssq", t)], writes=[("ve", t)])
            A("pool", lambda e: e.tensor_tensor(rstd[:, t:t + 1], ve_[:, t:t + 1], mhalf[:, 0:1], ALU.pow),
              reads=[("ve", t), "mhalf"], writes=[("rstd", t)])

        def pb_copy(t):
            xs_ = xsB[t % NXB]; xn_ = xn[t % 4]
            if t < 4:
                if t % 2 == 0:
                    A("act", lambda e: e.activation(xn_[:, :], xs_, AF.Copy, scale=rstd[:, t:t + 1]),
                      reads=[("xsB", t % NXB), ("rstd", t)], writes=[("xn", t % 4)])
                else:
                    A("pool", lambda e: e.tensor_scalar(xn_[:, :], xs_, rstd[:, t:t + 1], 0.0, ALU.mult, ALU.add),
                      reads=[("xsB", t % NXB), ("rstd", t)], writes=[("xn", t % 4)])
            else:
                A("act", lambda e: e.activation(xn_[:, 0:512], xs_[:, 0:512], AF.Copy, scale=rstd[:, t:t + 1]),
                  reads=[("xsB", t % NXB), ("rstd", t)], writes=[("xn", t % 4, 0)])
                A("pool", lambda e: e.tensor_scalar(xn_[:, 512:1024], xs_[:, 512:1024], rstd[:, t:t + 1], 0.0, ALU.mult, ALU.add),
                  reads=[("xsB", t % NXB), ("rstd", t)], writes=[("xn", t % 4, 1)])
            x_load(t + NXB)

        def pb_tr(t):
            xn_ = xn[t % 4]; pb = PX if t % 2 == 0 else PSK

            def tr(e):
                for fc in range(8):
                    ins = e.transpose(bkb(pb)[:, fc * 128:(fc + 1) * 128], xn_[:, fc * 128:(fc + 1) * 128], ident)
                return ins
            A("pe", tr, reads=[("xn", t % 4), ("xn", t % 4, 0), ("xn", t % 4, 1), "cb"],
              writes=[("ps", pb, 0), ("ps", pb, 1), ("ps", pb)])
            A("dve", lambda e: e.tensor_tensor(
                hT[:, :, t * 128:(t + 1) * 128],
                bkb(pb).rearrange("p (a b) -> p a b", a=8),
                col_bc(gpreT, 128), ALU.mult),
              reads=[("ps", pb, 0), ("ps", pb, 1), ("ps", pb), "vf"], writes=[("hT", t)])

        def rope_dve_ops(b):
            cs = slice(b * 512, (b + 1) * 512)
            t0, t1, t2, t3, t4 = [r_[:, :] for r_ in rt]
            rd = ["ropetmp"]; wr = ["ropetmp"]
            return [
                lambda: A("dve", lambda e: e.tensor_copy(t0, posi[:, cs]), reads=[("posi", b)] + rd, writes=wr),
                lambda: A("dve", lambda e: e.tensor_scalar(t2, t0, invf, None, ALU.mult), reads=["cf"] + rd, writes=wr),
                lambda: A("dve", lambda e: e.scalar_tensor_tensor(t1, t0, invf_lo, t2, ALU.mult, ALU.add), reads=["cf"] + rd, writes=wr),
                lambda: A("dve", lambda e: e.tensor_scalar(rki[:, :], t1, 1.0 / TWO_PI, None, ALU.mult), reads=rd, writes=wr),
                lambda: A("dve", lambda e: e.tensor_copy(t0, rki[:, :]), reads=rd, writes=wr),
                lambda: A("dve", lambda e: e.scalar_tensor_tensor(t2, t0, -CW1, t1, ALU.mult, ALU.add), reads=rd, writes=wr),
                lambda: A("dve", lambda e: e.scalar_tensor_tensor(t1, t0, -CW2, t2, ALU.mult, ALU.add), reads=rd, writes=wr),
                lambda: A("dve", lambda e: e.tensor_scalar(t3, t1, PI_SAFE, -PI_SAFE, ALU.min, ALU.max), reads=rd, writes=["ropetmp", "rt3"]),
                lambda: A("dve", lambda e: e.scalar_tensor_tensor(t4, t1, -1.0, t1, ALU.mult, ALU.max), reads=rd, writes=["ropetmp", "rt4"]),
            ]

        def rope_dve(b):
            for f in rope_dve_ops(b):
                f()

        def rope_act(b):
            cs = slice(b * 512, (b + 1) * 512)
            t0, t1, t2, t3, t4 = [r_[:, :] for r_ in rt]
            A("act", lambda e: e.activation(sinT[0:64, cs], t3[0:64, :], AF.Sin, scale=-1.0),
              reads=["ropetmp", "rt3"], writes=[("sinT", b, 0)])
            A("act", lambda e: e.activation(sinT[64:128, cs], t3[64:128, :], AF.Sin),
              reads=["ropetmp", "rt3"], writes=[("sinT", b, 1)])
            A("act", lambda e: e.activation(cosT[:, cs], t4, AF.Sin, bias=halfpi[:, 0:1], scale=-1.0),
              reads=["ropetmp", "rt4", "halfpi"], writes=[("cosT", b)])

        def rope_block(b):
            rope_dve(b)
            rope_act(b)

        NU = H * NTB

        pwt = xn01[0]
        pwt2 = xn01[1]
        def pwv(g, c2, dd):
            t_ = pwt if g < 2 else pwt2
            base = ((g % 2) * 2 + c2) * 256 + dd * 128
            return t_[:, base:base + 128]

        def load_poolw():
            for gg in range(2):
                t_ = pwt if gg == 0 else pwt2
                src = bass.AP(pool_w, gg * 512 * 256, [[256, 128], [128 * 256, 4], [1, 256]])
                A("pool", lambda e, t_=t_, src=src: e.dma_start(out=t_[:, :].rearrange("p (a b) -> p a b", a=4), in_=src),
                  writes=[("pw", gg), ("xn", gg)], dma_sem=d_pw[gg])

        UCOL = 4 * D
        GCOL = 5 * D
        evq = [0]
        marks = {}

        def evac_copy(dst, src, reads, writes, after=()):
            evq[0] += 1
            if evq[0] % 2 == 0:
                A("act", lambda e: e.activation(dst, src, AF.Copy), reads=reads, writes=writes, after=after)
            else:
                A("dve", lambda e: e.tensor_copy(dst, src), reads=reads, writes=writes, after=after)

        def u_tile(t):
            pb = t % 4

            def fu(e):
                for kc in range(8):
                    ins = e.matmul(bkf(pb), hT[:, kc, t * 128:(t + 1) * 128], wbuf[0][:, kc, :],
                                   start=(kc == 0), stop=(kc == 7))
                return ins
            A("pe", fu, reads=[("wbuf", 0), ("hT", t)], writes=[("ps", pb)])
            evac_copy(u_t[t], bkf(pb), [("ps", pb)], [("u", t)], after=marks["pb"])

        def unit(i):
            return i // NTB, i % NTB

        def s1(i, part):
            h, tb = unit(i)
            cs = slice(tb * 512, (tb + 1) * 512)
            wb = wbuf[h % 2]
            hdeps = [("hT", t) for t in range(tb * 4, tb * 4 + 4)]

            def proj(bk, c0):
                def f(e):
                    for kc in range(8):
                        ins = e.matmul(bkf(bk), wb[:, kc, c0:c0 + 128], hT[:, kc, cs], start=(kc == 0), stop=(kc == 7))
                    return ins
                A("pe", f, reads=[("wbuf", h % 2)] + hdeps, writes=[("ps", bk)])
            r2 = i % 2
            nxt = [4 * (i + 1) + j for j in range(4)] if i + 1 < NTB else []

            def hook(j):
                if nxt:
                    pb_sq(nxt[j] + 2)
                    pb_copy(nxt[j])
                    ops = rope_dve_ops(i + 1)
                    for f in ops[3 * j:3 * j + 3]:
                        f()
            if part == 0:
                if 1 <= i < NTB:
                    for hf in range(2):
                        cs2 = slice(tb * 512 + hf * 256, tb * 512 + (hf + 1) * 256)

                        def fh(e, hf=hf, cs2=cs2):
                            for kc in range(8):
                                ins = e.matmul(bkf(PB0)[:, hf * 256:(hf + 1) * 256], wb[:, kc, 0:128], hT[:, kc, cs2],
                                               start=(kc == 0), stop=(kc == 7))
                            return ins
                        A("pe", fh, reads=[("wbuf", h % 2)] + [("hT", tb * 4 + 2 * hf), ("hT", tb * 4 + 2 * hf + 1)],
                          writes=[("ps", PB0)])
                else:
                    proj(PB0, 0)
                A("act", lambda e: e.activation(qraw[r2], bkf(PB0), AF.Copy), reads=[("ps", PB0)], writes=[("qraw", r2)])
                A("dve", lambda e: e.tensor_tensor(aq, bkf(PB0), cosT[:, cs], ALU.mult),
                  reads=[("ps", PB0), ("cosT", tb)], writes=["aq"])
                hook(0)
            elif part == 1:
                proj(PB1, 128)
                A("act", lambda e: e.activation(kraw[r2], bkf(PB1), AF.Copy), reads=[("ps", PB1)], writes=[("kraw", r2)])
                A("dve", lambda e: e.tensor_tensor(ak2[r2], bkf(PB1), cosT[:, cs], ALU.mult),
                  reads=[("ps", PB1), ("cosT", tb)], writes=[("ak", r2)])
                hook(1)
            elif part == 2:
                proj(PBV, 256)
                A("act", lambda e: e.activation(vTb[r2], bkf(PBV), AF.Copy), reads=[("ps", PBV)], writes=[("vTb", r2)])
                hook(2)
            else:
                proj(PBG, 384)
                sg = sgr[i % NSG]
                A("act", lambda e: e.activation(sg, bkf(PBG), AF.Silu), reads=[("ps", PBG)], writes=[("sgr", i % NSG)])
                hook(3)
                if nxt:
                    rope_act(i + 1)
                    for t_ in nxt:
                        pb_tr(t_)
                if tb == NTB - 1 and h + 2 < H:
                    load_wblock(h % 2, (h + 2) * 512, 512, d_w[h % 2])
                if tb == NTB - 1 and h == H - 2:
                    load_wblock(0, 4 * D, 512, d_w[0])
                if tb == NTB - 1 and h == H - 1:
                    load_wblock(1, 5 * D, 512, d_w[1])
                if i == 8:
                    load_poolw()
                if i == 1:
                    load_wblock(1, 512, 512, d_w[1])
                if i == 3:
                    A("pool", lambda e: e.dma_start(out=cb[:, 384:1920], in_=constb[:, 384:1920]), writes=["cbA"],
                      dma_sem=d_c[2])

        def s2q(i):
            h, tb = unit(i)
            cs = slice(tb * 512, (tb + 1) * 512)
            r2 = i % 2
            A("pe", lambda e: e.matmul(bkf(PSW), Pm, qraw[r2], start=True, stop=True),
              reads=[("qraw", r2), "cb"], writes=[("ps", PSW)])
            A("dve", lambda e: e.tensor_tensor(bq, bkf(PSW), sinT[:, cs], ALU.mult),
              reads=[("ps", PSW), ("sinT", tb, 0), ("sinT", tb, 1)], writes=["bq"])
            A("pool", lambda e: e.tensor_tensor(qT[i % NQ], aq, bq, ALU.add),
              reads=["aq", "bq"], writes=[("qT", i % NQ)])

        def s2k(i):
            h, tb = unit(i)
            cs = slice(tb * 512, (tb + 1) * 512)
            r2 = i % 2
            A("pe", lambda e: e.matmul(bkf(PSW), Pm, kraw[r2], start=True, stop=True),
              reads=[("kraw", r2), "cb"], writes=[("ps", PSW)])
            A("dve", lambda e: e.tensor_tensor(bk, bkf(PSW), sinT[:, cs], ALU.mult),
              reads=[("ps", PSW), ("sinT", tb, 0), ("sinT", tb, 1)], writes=["bk"])
            A("pool", lambda e: e.tensor_tensor(kT[i % NK], ak2[r2], bk, ALU.add),
              reads=[("ak", r2), "bk"], writes=[("kT", i % NK)])

        def s3a(i):
            h, tb = unit(i)
            r2 = i % 2
            vt = vTb[r2]; k_ = kT[i % NK]; q_ = qT[i % NQ]

            def sc(e):
                for j in range(4):
                    ins = e.matmul(bkf(PSK)[:, j * 128:(j + 1) * 128], k_[:, j * 128:(j + 1) * 128],
                                   q_[:, j * 128:(j + 1) * 128], start=True, stop=True)
                return ins
            A("pe", sc, reads=[("kT", i % NK), ("qT", i % NQ)], writes=[("ps", PSK)])
            A("dve", lambda e: e.scalar_tensor_tensor(
                SsT[r2].rearrange("p (a b) -> p a b", a=4),
                bkf(PSK).rearrange("p (a b) -> p a b", a=4),
                kscale(h), bc(mask, 4), ALU.mult, ALU.mult),
              reads=[("ps", PSK), "cf", "cb"], writes=[("SsT", r2)])

            def trv(e):
                for j in range(4):
                    ins = e.transpose(bkb(PX)[:, j * 128:(j + 1) * 128], vt[:, j * 128:(j + 1) * 128], ident)
                return ins
            A("pe", trv, reads=[("vTb", r2), "cb"], writes=[("ps", PX, 0)])
            A("act", lambda e: e.activation(vsb[i % NV].rearrange("p a b -> p (a b)"), bkb(PX)[:, 0:512], AF.Copy),
              reads=[("ps", PX, 0)], writes=[("vsb", i % NV)])

        def s3b(i):
            h, tb = unit(i)
            r2 = i % 2
            k_ = kT[i % NK]

            def trk(e):
                for j in range(4):
                    ins = e.transpose(bkb(PX)[:, 512 + j * 128:512 + (j + 1) * 128], k_[:, j * 128:(j + 1) * 128], ident)
                return ins
            A("pe", trk, reads=[("kT", i % NK), "cb"], writes=[("ps", PX, 1)])
            A("act", lambda e: e.activation(ktl[r2].rearrange("p a b -> p (a b)"), bkb(PX)[:, 512:1024], AF.Copy,
                                            scale=kscale(h)),
              reads=[("ps", PX, 1), "cf"], writes=[("kt", r2)])

        def s4(i):
            h, tb = unit(i)
            r2 = i % 2
            kt_ = ktl[r2]; v_ = vsb[i % NV]; Rb = Rbf[i % NR]

            def kv(e):
                for j in range(4):
                    ins = e.matmul(bkf(PKV)[:, j * 128:(j + 1) * 128], kt_[:, j, :], v_[:, j, :], start=True, stop=True)
                return ins
            A("pe", kv, reads=[("kt", r2), ("vsb", i % NV)], writes=[("ps", PKV)])
            for j in range(4):
                c = tb * 4 + j
                pk = bkf(PKV)[:, j * 128:(j + 1) * 128]
                Tn = Tst[h % 2][c % 2]; Tp = Tst[h % 2][(c - 1) % 2]
                if c == 0:
                    A("dve", lambda e, pk=pk, Tn=Tn: e.tensor_copy(Tn[:, :], pk),
                      reads=[("ps", PKV)], writes=[("T", h % 2, c % 2)])
                else:
                    A("dve", lambda e, pk=pk, Tn=Tn, Tp=Tp: e.scalar_tensor_tensor(Tn[:, :], Tp[:, :], g128[h], pk,
                                                                                  ALU.mult, ALU.add),
                      reads=[("ps", PKV), ("T", h % 2, (c - 1) % 2)], writes=[("T", h % 2, c % 2)])
                A("pool", lambda e, j=j, Tn=Tn: e.tensor_scalar(Rb[:, j, :], Tn[:, :], g128[h], 0.0, ALU.mult, ALU.add),
                  reads=[("T", h % 2, c % 2)], writes=[("Rbf", i % NR, j)])

        def s5a(i):
            h, tb = unit(i)
            r2 = i % 2
            v_ = vsb[i % NV]; q_ = qT[i % NQ]; S_ = SsT[r2]
            rdeps = [("SsT", r2), ("vsb", i % NV), ("qT", i % NQ)] + [("Rbf", i % NR, j) for j in range(3)]
            if tb > 0:
                rdeps.append(("Rbf", (i - 1) % NR, 3))

            def om(e):
                for j in range(4):
                    po = bkf(PO)[:, j * 128:(j + 1) * 128]
                    first = (tb == 0 and j == 0)
                    ins = e.matmul(po, S_[:, j * 128:(j + 1) * 128], v_[:, j, :], start=True, stop=first)
                    if not first:
                        Rprev = Rbf[i % NR][:, j - 1, :] if j > 0 else Rbf[(i - 1) % NR][:, 3, :]
                        ins = e.matmul(po, q_[:, j * 128:(j + 1) * 128], Rprev, start=False, stop=True)
                return ins
            A("pe", om, reads=rdeps, writes=[("ps", PO)])
            st = bst[r2]; mv = bmv[r2]
            for j in range(4):
                A("dve", lambda e, j=j: e.bn_stats(st[:, j, :], bkf(PO)[:, j * 128:(j + 1) * 128]),
                  reads=[("ps", PO)], writes=[("bst", r2, j)])
                A("dve", lambda e, j=j: e.bn_aggr(mv[:, j, :], st[:, j, :]),
                  reads=[("bst", r2, j)], writes=[("bmv", r2, j)])
            mvd = [("bmv", r2, j) for j in range(4)]
            A("dve", lambda e: e.tensor_scalar(gve[r2][:, :], mv[:, :, 1], epsc(h), None, ALU.add),
              reads=mvd + ["cf"], writes=[("gve", r2)])
            A("pool", lambda e: e.tensor_tensor(grs[r2][:, :], gve[r2][:, :], mhalf[:, 0:4], ALU.pow),
              reads=[("gve", r2), "mhalf"], writes=[("grs", r2)])

        def s5b(i):
            h, tb = unit(i)
            r2 = i % 2
            mv = bmv[r2]
            mvd = [("bmv", r2, j) for j in range(4)]
            A("dve", lambda e: e.scalar_tensor_tensor(gnb[r2][:, :], mv[:, :, 0], -1.0, grs[r2][:, :], ALU.mult, ALU.mult),
              reads=mvd + [("grs", r2)], writes=[("gnb", r2)])
            for j in range(4):
                A("act", lambda e, j=j: e.activation(onb[r2][:, j, :], bkf(PO)[:, j * 128:(j + 1) * 128], AF.Identity,
                                                     bias=gnb[r2][:, j:j + 1], scale=grs[r2][:, j:j + 1]),
                  reads=[("ps", PO), ("gnb", r2), ("grs", r2)], writes=[("on", r2, j)])

        first_y = [True]

        def s6(i):
            h, tb = unit(i)
            r2 = i % 2
            cs = slice(tb * 512, (tb + 1) * 512)
            on_ = onb[r2]

            def tro(e):
                for j in range(4):
                    ins = e.transpose(bkb(POT)[:, j * 128:(j + 1) * 128], on_[:, j, :], ident)
                return ins
            A("pe", tro, reads=[("on", r2, j) for j in range(4)] + ["cb"], writes=[("ps", POT, 0)])
            wr = [("yT", h, tb)]
            if first_y[0]:
                wr += ["ropetmp"] + [("posi", b) for b in range(4)]
                first_y[0] = False
            A("dve", lambda e: e.scalar_tensor_tensor(yT[:, h, cs], bkb(POT)[:, 0:512], normg(h), sgr[i % NSG],
                                                      ALU.mult, ALU.mult),
              reads=[("ps", POT, 0), "vf", ("sgr", i % NSG)], writes=wr)

        for i in range(NU + 5):
            if i == 0:
                rope_dve(0)
                pb_sq(0)
                pb_sq(1)
                for t in range(4):
                    pb_copy(t)
                    if t + 2 < 4:
                        pb_sq(t + 2)
                    pb_tr(t)
                pb_sq(4)
                pb_sq(5)
                rope_act(0)
            ok = lambda u: 0 <= u < NU
            if ok(i - 1): s2q(i - 1)
            if ok(i - 5): s6(i - 5)
            if ok(i - 4): s5a(i - 4)
            dr = i - NU
            def big(part):
                if i < NU:
                    s1(i, part)
                elif dr < 4:
                    u_tile(4 * dr + part)
            if ok(i - 2): s3a(i - 2)
            big(0)
            big(1)
            if ok(i - 2): s3b(i - 2)
            if ok(i - 4): s5b(i - 4)
            if ok(i - 3): s4(i - 3)
            big(2)
            if ok(i - 1): s2k(i - 1)
            big(3)
            if i == NTB - 1:
                marks["pb"] = Sc.mark()

        ret_done = Sc.mark()

        for q4 in range(4):
            src = bass.AP(w_out, q4 * 512 * D, [[D, 128], [128 * D, 4], [1, D]])
            A("pool", lambda e, q4=q4, src=src: e.dma_start(out=wsb[:, q4 * 4:(q4 + 1) * 4, :], in_=src),
              writes=[("wsb", q4)], dma_sem=d_wo, after=ret_done)

        for half in range(2):
            if half == 1:
                load_wblock(0, UCOL + half * 512, 512, d_w[0])
                load_wblock(1, GCOL + half * 512, 512, d_w[1])
            if half == 1:
                for t in range(NT):
                    u_tile(t)
            for tb in range(NTB):
                for cc in range(4):
                    g = 2 * half + cc // 2
                    pbk = 2 + (tb * 4 + cc) % 2
                    def fm(e, tb=tb, cc=cc, g=g, pbk=pbk):
                        for tt in range(4):
                            t = tb * 4 + tt
                            o_ = bkf(pbk)[:, tt * 128:(tt + 1) * 128]
                            ins = e.matmul(o_, u_t[t][:, cc * 128:(cc + 1) * 128], Amat(g, 0 if t == 0 else 1),
                                           start=True, stop=(t == 0))
                            if t > 0:
                                ins = e.matmul(o_, u_t[t - 1][:, cc * 128:(cc + 1) * 128], Amat(g, 2),
                                               start=False, stop=True)
                        return ins
                    rd = [("u", t) for t in range(max(0, tb * 4 - 1), tb * 4 + 4)] + ["cbA"]
                    A("pe", fm, reads=rd, writes=[("ps", pbk)])
                    evac_copy(mixT[:, cc, tb * 512:(tb + 1) * 512], bkf(pbk), [("ps", pbk)], [("mix", cc, tb)], after=ret_done)
            for f in range(4):
                g = 2 * half + f // 2
                dd = f % 2
                for tb in range(NTB):
                    cs = slice(tb * 512, (tb + 1) * 512)
                    n_ = (f * 4 + tb)
                    pg = 4 + n_ % 2
                    pp = 6 + n_ % 2
                    def fg(e, f=f, cs=cs, pg=pg):
                        for kc in range(8):
                            ins = e.matmul(bkf(pg), wbuf[1][:, kc, f * 128:(f + 1) * 128], hT[:, kc, cs],
                                           start=(kc == 0), stop=(kc == 7))
                        return ins
                    A("pe", fg, reads=[("wbuf", 1)] + [("hT", t) for t in range(tb * 4, tb * 4 + 4)],
                      writes=[("ps", pg)])
                    sg = sgp[n_ % 3]
                    A("act", lambda e, sg=sg, pg=pg: e.activation(sg, bkf(pg), AF.Silu),
                      reads=[("ps", pg)], writes=[("sgp", n_ % 3)])
                    def fp(e, g=g, dd=dd, f=f, cs=cs, pp=pp):
                        for c2 in range(2):
                            ins = e.matmul(bkf(pp), pwv(g, c2, dd), mixT[:, 2 * (f // 2) + c2, cs],
                                           start=(c2 == 0), stop=(c2 == 1))
                        return ins
                    A("pe", fp, reads=[("pw", g // 2), ("mix", 2 * (f // 2), tb), ("mix", 2 * (f // 2) + 1, tb)],
                      writes=[("ps", pp)])
                    ych = 8 + half * 4 + f
                    A("dve", lambda e, ych=ych, cs=cs, pp=pp, sg=sg: e.scalar_tensor_tensor(
                        yT[:, ych, cs], bkf(pp), pscale(ych - 8), sg, ALU.mult, ALU.mult),
                      reads=[("ps", pp), "vf", ("sgp", n_ % 3)], writes=[("yT", ych, tb)], after=marks["pb"])

        Sc.barrier()

        A("sp", lambda e: e.dma_start(out=gpb, in_=bass.AP(gpost, 0, [[0, 128], [1, D]])), writes=["gpb"], dma_sem=d_c[4])
        for t in range(NT):
            xs_ = xst[t % 3]
            A("sp", lambda e, t=t, xs_=xs_: e.dma_start(out=xs_[:, :], in_=x[t * 128:(t + 1) * 128, :]),
              writes=[("xst", t % 3)], dma_sem=d_x[t % 3])
            b0 = (t % 2) * 2
            def fo(e, t=t, b0=b0):
                for n in range(2):
                    for fch in range(16):
                        ins = e.matmul(bkf(b0 + n), yT[:, fch, t * 128:(t + 1) * 128], wsb[:, fch, n * 512:(n + 1) * 512],
                                       start=(fch == 0), stop=(fch == 15))
                return ins
            A("pe", fo, reads=[("wsb", q4) for q4 in range(4)] + [("yT", c, t // 4) for c in range(16)],
              writes=[("ps", b0), ("ps", b0 + 1)])
            r2 = t % 2
            for n in range(2):
                A("act", lambda e, n=n, b0=b0, r2=r2: e.activation(ftmp[r2][:, n * 512:(n + 1) * 512], bkf(b0 + n), AF.Square,
                                                                   accum_out=fss[r2][:, n:n + 1]),
                  reads=[("ps", b0 + n)], writes=[("ftmp", r2, n), ("fss", r2, n)])
            A("dve", lambda e, r2=r2: e.tensor_tensor(fss[r2][:, 2:3], fss[r2][:, 0:1], fss[r2][:, 1:2], ALU.add),
              reads=[("fss", r2, 0), ("fss", r2, 1)], writes=[("fss", r2, 2)])
            A("dve", lambda e, r2=r2: e.tensor_scalar(fss[r2][:, 3:4], fss[r2][:, 2:3], 1.0 / D, EPS, ALU.mult, ALU.add),
              reads=[("fss", r2, 2)], writes=[("fss", r2, 3)])
            A("pool", lambda e, r2=r2: e.tensor_tensor(frs[r2][:, 0:1], fss[r2][:, 3:4], mhalf[:, 0:1], ALU.pow),
              reads=[("fss", r2, 3), "mhalf"], writes=[("frs", r2)])
            for n in range(2):
                hs = slice(n * 512, (n + 1) * 512)
                A("dve", lambda e, n=n, b0=b0, r2=r2, hs=hs: e.scalar_tensor_tensor(
                    ftmp[r2][:, hs], bkf(b0 + n), frs[r2][:, 0:1], gpb[:, hs],
                    ALU.mult, ALU.mult),
                  reads=[("ps", b0 + n), ("frs", r2), "gpb"], writes=[("ftmp", r2, n)])
                r3 = t % 3
                A("dve", lambda e, r2=r2, r3=r3, xs_=xs_, hs=hs: e.tensor_tensor(fout[r3][:, hs], ftmp[r2][:, hs], xs_[:, hs], ALU.add),
                  reads=[("ftmp", r2, n), ("xst", t % 3)], writes=[("fout", r3, n)])
                A("sp", lambda e, t=t, r3=r3, hs=hs: e.dma_start(out=out[t * 128:(t + 1) * 128, hs], in_=fout[r3][:, hs]),
                  reads=[("fout", r3, n)], dma_sem=d_o[r3][n])

        Sc.emit(sems, final_waits=[(d_o[a][b], 16 * len([t for t in range(NT) if t % 3 == a])) for a in range(3) for b in range(2)])
    return nc


def _consts():
    gam = 1.0 - 2.0 ** (-5.0 - np.arange(H, dtype=np.float64))
    s = np.arange(128, dtype=np.float64)
    cf = np.zeros((128, 24), np.float64)
    cf[:, 0:8] = gam[None, :] ** (-(s[:, None] + 1.0)) * (128.0 ** -0.5)
    cf[:, 8:16] = EPS * gam[None, :] ** (-2.0 * (s[:, None] + 1.0))
    half = 64
    inv_freq = 10000.0 ** (-np.arange(half, dtype=np.float64) / half)
    f_hi = inv_freq.astype(np.float32).astype(np.float64)
    f_lo = inv_freq - f_hi
    cf[:, 16] = np.concatenate([f_hi, f_hi])
    cf[:, 17] = np.concatenate([f_lo, f_lo])
    cb = np.zeros((128, 15 * 128), np.float64)
    cb[:, 0:128] = np.eye(128)
    P = np.zeros((128, 128))
    for m in range(128):
        P[(m + 64) % 128, m] = 1.0
    cb[:, 128:256] = P
    ss, cc = np.meshgrid(np.arange(128), np.arange(128), indexing="ij")
    cb[:, 256:384] = (cc >= ss).astype(np.float64)
    for g, w in enumerate(WINDOWS):
        t = cc; s_ = ss
        cnt = np.minimum(t + 1, w).astype(np.float64)
        a0 = ((s_ <= t) & (s_ > t - w)) / cnt - (s_ == t)
        a1 = ((s_ <= t) & (s_ > t - w)) / float(w) - (s_ == t)
        a2 = ((s_ - 128) > (t - w)) / float(w)
        cb[:, (3 + 3 * g + 0) * 128:(3 + 3 * g + 1) * 128] = a0
        cb[:, (3 + 3 * g + 1) * 128:(3 + 3 * g + 2) * 128] = a1
        cb[:, (3 + 3 * g + 2) * 128:(3 + 3 * g + 3) * 128] = a2
    return cf.astype(np.float32), cb.astype(np.float32)


_PROGRAM = None


def kernel(x, positions, w_in, w_out, pool_w, pool_scale, ret_norm_g, pre_norm_g, post_norm_g):
    global _PROGRAM
    x = np.asarray(x); positions = np.asarray(positions)
    w_in = np.asarray(w_in)[0]; w_out = np.asarray(w_out)[0]; pool_w = np.asarray(pool_w)[0]
    B = x.shape[0]
    assert B == 8 and x.shape[1] == S and x.shape[2] == D
    blocks = []
    for h in range(H):
        for p in (0, 1, 2, 3):
            blocks.append(w_in[:, p * D + h * 128: p * D + (h + 1) * 128])
    blocks.append(w_in[:, 4 * D:5 * D])
    blocks.append(w_in[:, 5 * D:6 * D])
    w_in_r = np.ascontiguousarray(np.concatenate(blocks, axis=1), dtype=np.float32)
    vecs = np.zeros((128, 24), np.float32)
    vecs[:, 0:8] = np.asarray(pre_norm_g)[0].reshape(8, 128).T
    vecs[:, 8:16] = np.asarray(ret_norm_g)[0].reshape(8, 128).T
    vecs[:, 16:24] = np.asarray(pool_scale)[0].reshape(8, 128).T
    gpost = np.ascontiguousarray(np.asarray(post_norm_g)[0].reshape(1, D), dtype=np.float32)
    cf, cb = _consts()
    if _PROGRAM is None:
        _PROGRAM = build_program()
    nc = _PROGRAM
    in_maps = []
    for b in range(B):
        in_maps.append({
            "x": np.ascontiguousarray(x[b], dtype=np.float32),
            "pos": np.ascontiguousarray(positions[b].reshape(1, S), dtype=np.int32),
            "w_in": w_in_r,
            "w_out": np.ascontiguousarray(w_out, dtype=np.float32),
            "pool_w": np.ascontiguousarray(pool_w.reshape(4 * 256, 256), dtype=np.float32),
            "vecs": vecs, "gpost": gpost, "constf": cf, "constb": cb,
        })
    res = run_bass_kernel_spmd(nc, in_maps, core_ids=list(range(B)))
    return np.stack([np.asarray(r["out"]).reshape(S, D) for r in res.results], axis=0).astype(np.float32)


if __name__ == "__main__":
    import time
    t0 = time.time()
    nc = build_program()
    print("built in", time.time() - t0)
```

```python
import math
from contextlib import ExitStack

import numpy as np
import concourse.bass as bass
import concourse.mybir as mybir
from concourse.bass_utils import run_bass_kernel_spmd

F32 = mybir.dt.float32
BF16 = mybir.dt.bfloat16
I32 = mybir.dt.int32
AF = mybir.ActivationFunctionType
ALU = mybir.AluOpType

D = 1024
S = 2048
H = 8
NT = 16
NTB = 4
EPS = 1e-6
WINDOWS = (2, 4, 8, 16)
TWO_PI = 2.0 * math.pi
CW1 = 6.28125
CW2 = TWO_PI - CW1
PI_SAFE = 3.1415925


class _Op:
    __slots__ = ("eng", "fn", "deps", "is_dma", "dma_sem", "dma_val", "signal", "count", "name", "idx", "waits", "know")


class Sched:
    ENGS = ("pe", "act", "dve", "pool", "sp")

    def __init__(self, nc):
        self.nc = nc
        self.ops = []
        self.last_writer = {}
        self.readers = {}
        self.dma_sem_count = {}
        self.barrier_deps = []
        self.last_of_eng = {}
        self.last_of_dsem = {}
        self.excl_last = {}

    def add(self, eng, fn, reads=(), writes=(), dma_sem=None, name="", after=()):
        op = _Op()
        op.eng = eng; op.fn = fn; op.name = name
        op.is_dma = dma_sem is not None
        op.dma_sem = dma_sem
        op.signal = False; op.count = 0; op.dma_val = 0
        op.idx = len(self.ops)
        deps = set(self.barrier_deps)
        deps.update(after)
        for r in reads:
            w = self.last_writer.get(r)
            if w is not None:
                deps.add(w)
        for w_ in writes:
            lw = self.last_writer.get(w_)
            if lw is not None:
                deps.add(lw)
            for rd in self.readers.get(w_, ()):
                deps.add(rd)
        banks = set()
        for r in list(reads) + list(writes):
            if isinstance(r, tuple) and r[0] == "ps":
                banks.add(r[1])
        for b in banks:
            g = self.excl_last.get(b)
            if g is not None and g[0] != eng:
                deps.add(g[1])
            self.excl_last[b] = (eng, op)
        deps.discard(op)
        op.deps = deps
        for r in reads:
            self.readers.setdefault(r, []).append(op)
        for w_ in writes:
            self.last_writer[w_] = op
            self.readers[w_] = []
        if op.is_dma:
            c = self.dma_sem_count.get(id(dma_sem), 0) + 16
            self.dma_sem_count[id(dma_sem)] = c
            op.dma_val = c
            self.last_of_dsem[id(dma_sem)] = op
        else:
            self.last_of_eng[eng] = op
        self.ops.append(op)
        return op

    def mark(self):
        return list(self.last_of_eng.values()) + list(self.last_of_dsem.values())

    def barrier(self):
        self.barrier_deps = list(self.last_of_eng.values()) + list(self.last_of_dsem.values())

    def emit(self, sems, final_waits=()):
        nc = self.nc
        for op in self.ops:
            for d in op.deps:
                if d.is_dma:
                    continue
                if d.eng == "pe" and op.eng == "pe" and not op.is_dma:
                    continue
                d.signal = True
        counts = {e: 0 for e in self.ENGS}
        for op in self.ops:
            if op.is_dma:
                continue
            if op.signal:
                counts[op.eng] += 1
                op.count = counts[op.eng]
        per_eng = {e: [o for o in self.ops if o.eng == e] for e in self.ENGS}

        eng_know = {e: {} for e in self.ENGS}
        self.n_waits = 0
        self.n_skipped = 0
        for op in self.ops:
            need = {}
            for d in op.deps:
                if d.is_dma:
                    key = ("dma", id(d.dma_sem)); sem = d.dma_sem; val = d.dma_val
                else:
                    if d.eng == "pe" and op.eng == "pe" and not op.is_dma:
                        continue
                    key = d.eng; sem = sems[d.eng]; val = d.count
                if val > need.get(key, (None, 0, None))[1]:
                    need[key] = (sem, val, d)
            know = eng_know[op.eng]
            waits = []
            for key, (sem, val, d) in sorted(need.items(), key=lambda kv: -kv[1][1]):
                if know.get(key, 0) >= val:
                    self.n_skipped += 1
                    continue
                waits.append((sem, val))
                self.n_waits += 1
                know[key] = val
                for k2, v2 in d.know.items():
                    if v2 > know.get(k2, 0):
                        know[k2] = v2
            op.waits = waits
            op.know = dict(know)
            if op.is_dma:
                op.know[("dma", id(op.dma_sem))] = op.dma_val
            elif op.signal:
                op.know[op.eng] = max(op.know.get(op.eng, 0), op.count)

        def run_stream(e, engobj):
            for op in per_eng[e]:
                for sem, val in op.waits:
                    engobj.wait_ge(sem, val)
                ins = op.fn(engobj)
                if op.is_dma:
                    ins.then_inc(op.dma_sem, 16)
                elif op.signal:
                    ins.then_inc(sems[op.eng], 1)
            if e == "sp":
                for sem, val in final_waits:
                    engobj.wait_ge(sem, val)

        with nc.Block() as block:
            @block.tensor
            def _(eng):
                run_stream("pe", eng)

            @block.scalar
            def _(eng):
                run_stream("act", eng)

            @block.vector
            def _(eng):
                run_stream("dve", eng)

            @block.gpsimd
            def _(eng):
                run_stream("pool", eng)

            @block.sync
            def _(eng):
                run_stream("sp", eng)


def bc(ap2d, n_mid):
    (ps, pn), (st, n) = ap2d.ap
    return bass.AP(ap2d.tensor, ap2d.offset, [[ps, pn], [0, n_mid], [st, n]])


def col_bc(ap2d, n_in):
    (ps, pn), (st, k) = ap2d.ap
    return bass.AP(ap2d.tensor, ap2d.offset, [[ps, pn], [st, k], [0, n_in]])


def build_program():
    nc = bass.Bass("TRN2", target_bir_lowering=False)
    x = nc.dram_tensor("x", [S, D], F32, kind="ExternalInput")
    pos = nc.dram_tensor("pos", [1, S], I32, kind="ExternalInput")
    w_in = nc.dram_tensor("w_in", [D, 6 * D], F32, kind="ExternalInput")
    w_out = nc.dram_tensor("w_out", [2 * D, D], F32, kind="ExternalInput")
    pool_w = nc.dram_tensor("pool_w", [4 * 256, 256], F32, kind="ExternalInput")
    vecs = nc.dram_tensor("vecs", [128, 24], F32, kind="ExternalInput")
    gpost = nc.dram_tensor("gpost", [1, D], F32, kind="ExternalInput")
    constf = nc.dram_tensor("constf", [128, 24], F32, kind="ExternalInput")
    constb = nc.dram_tensor("constb", [128, 15 * 128], F32, kind="ExternalInput")
    out = nc.dram_tensor("out", [S, D], F32, kind="ExternalOutput")

    gam = [1.0 - 2.0 ** (-5.0 - h) for h in range(H)]
    g128 = [g ** 128 for g in gam]

    with ExitStack() as ctx:
        def sb(name, shape, dt):
            return ctx.enter_context(nc.sbuf_tensor(name, shape, dt))

        def sem(name):
            return ctx.enter_context(nc.semaphore(name))

        hT = sb("hT", [128, 8, S], BF16)
        yT = sb("yT", [128, 16, S], BF16)
        RW = sb("RW", [128, 16384], BF16)
        RP = sb("RP", [128, 18432], BF16)
        wbuf = [sb(f"wbuf{i}", [128, 8, 512], BF16) for i in range(2)]
        xst = [sb(f"xst{i}", [128, D], F32) for i in range(3)]
        xn01 = [sb(f"xn{i}", [128, D], BF16) for i in range(2)]
        cb = sb("cb", [128, 15 * 128], BF16)
        cf = sb("cf", [128, 24], F32)
        vf = sb("vf", [128, 24], F32)
        Tst = [[sb(f"Tst{i}_{k}", [128, 128], F32) for k in range(2)] for i in range(2)]
        ssq = sb("ssq", [128, 16], F32)
        rstd = sb("rstd", [128, 16], F32)
        ve_ = sb("ve", [128, 16], F32)
        mhalf = sb("mhalf", [128, 16], F32)
        halfpi = sb("halfpi", [128, 1], F32)
        actwarm = sb("actwarm", [128, 1], F32)
        bst = [sb(f"bst{i}", [128, 4, 6], F32) for i in range(2)]
        bmv = [sb(f"bmv{i}", [128, 4, 2], F32) for i in range(2)]
        gve = [sb(f"gve{i}", [128, 4], F32) for i in range(2)]
        grs = [sb(f"grs{i}", [128, 4], F32) for i in range(2)]
        gnb = [sb(f"gnb{i}", [128, 4], F32) for i in range(2)]
        fss = [sb(f"fss{i}", [128, 4], F32) for i in range(2)]
        frs = [sb(f"frs{i}", [128, 2], F32) for i in range(2)]

        cosT = RW[:, 0:4096].bitcast(F32)
        sinT = RW[:, 4096:8192].bitcast(F32)
        aq = RW[:, 8192:9216].bitcast(F32)
        bq = RW[:, 9216:10240].bitcast(F32)
        ak2 = [RW[:, 10240:11264].bitcast(F32), RW[:, 11264:12288].bitcast(F32)]
        bk = RW[:, 12288:13312].bitcast(F32)
        NQ, NK = 4, 2
        qT = [RW[:, 13312 + 512 * i:13312 + 512 * (i + 1)] for i in range(NQ)]
        kT = [RW[:, 15360 + 512 * i:15360 + 512 * (i + 1)] for i in range(NK)]
        wsb = RW[:, :].rearrange("p (a b) -> p a b", a=16)

        def rp(off, n):
            return RP[:, off:off + n]
        o = 0
        qraw = [rp(o + 512 * i, 512) for i in range(2)]; o += 1024
        kraw = [rp(o + 512 * i, 512) for i in range(2)]; o += 1024
        vTb = [rp(o + 512 * i, 512) for i in range(2)]; o += 1024
        NSG = 7
        sgr = [rp(o + 512 * i, 512) for i in range(NSG)]; o += 512 * NSG
        NV = 4
        vsb = [rp(o + 512 * i, 512).rearrange("p (a b) -> p a b", a=4) for i in range(NV)]; o += 512 * NV
        ktl = [rp(o + 512 * i, 512).rearrange("p (a b) -> p a b", a=4) for i in range(2)]; o += 1024
        SsT = [rp(o + 512 * i, 512) for i in range(2)]; o += 1024
        NR = 3
        Rbf = [rp(o + 512 * i, 512).rearrange("p (a b) -> p a b", a=4) for i in range(NR)]; o += 512 * NR
        onb = [rp(o + 512 * i, 512).rearrange("p (a b) -> p a b", a=4) for i in range(2)]; o += 1024
        assert o <= 13312, o
        sqj = RP[:, 13312:14336]
        xn = [xn01[0][:, :], xn01[1][:, :], RP[:, 14336:15360], RP[:, 15360:16384]]
        u_t = []
        for k in range(3):
            xb_ = xst[k][:, :].bitcast(BF16)
            u_t += [xb_[:, 512 * j:512 * (j + 1)] for j in range(4)]
        u_t += [RP[:, 13312 + 512 * j:13312 + 512 * (j + 1)] for j in range(4)]
        mixT = RP[:, 0:8192].rearrange("p (a b) -> p a b", a=4)
        sgp = [RP[:, 15360 + 512 * i:15360 + 512 * (i + 1)] for i in range(3)]
        ftmp = [RP[:, 2048 * i:2048 * (i + 1)].bitcast(F32) for i in range(2)]
        fout = [RP[:, 4096 + 2048 * i:4096 + 2048 * (i + 1)].bitcast(F32) for i in range(3)]
        gpb = RP[:, 10240:12288].bitcast(F32)
        NXB = 8
        xsB = [yT[:, 8 + k, :].bitcast(F32) for k in range(NXB)]
        yflat = yT[:, :, :].rearrange("p a b -> p (a b)")
        posi = yflat[:, 0:4096].bitcast(I32)
        rt = [yflat[:, 4096 + 1024 * i:4096 + 1024 * (i + 1)].bitcast(F32) for i in range(5)]
        rki = yflat[:, 4096 + 5120:4096 + 6144].bitcast(I32)

        ident = cb[:, 0:128]
        Pm = cb[:, 128:256]
        mask = cb[:, 256:384]
        def Amat(g, k):
            c0 = (3 + 3 * g + k) * 128
            return cb[:, c0:c0 + 128]
        kscale = lambda h: cf[:, h:h + 1]
        epsc = lambda h: cf[:, 8 + h:9 + h]
        invf = cf[:, 16:17]
        invf_lo = cf[:, 17:18]
        gpreT = vf[:, 0:8]
        normg = lambda h: vf[:, 8 + h:9 + h]
        pscale = lambda c: vf[:, 16 + c:17 + c]

        bank = [ctx.enter_context(nc.psum_tensor(f"bank{i}", [128, 512], F32)) for i in range(8)]
        def bkf(i):
            return bank[i][:, :]
        def bkb(i):
            return bank[i][:, :].bitcast(BF16)

        sems = {e: sem("s_" + e) for e in ("pe", "act", "dve", "pool")}
        d_c = [sem(f"d_c{i}") for i in range(5)]
        d_p = [sem(f"d_p{i}") for i in range(4)]
        d_x = [sem(f"d_x{i}") for i in range(3)]
        d_xb = [sem(f"d_xb{i}") for i in range(8)]
        d_w = [sem(f"d_w{i}") for i in range(2)]
        d_wo = sem("d_wo")
        d_pw = [sem(f"d_pw{i}") for i in range(2)]
        d_o = [[sem(f"d_o{i}_{n}") for n in range(2)] for i in range(3)]

        Sc = Sched(nc)
        PBQ, PBK, PBV, PBG, PSW, PX, PSK, PO = range(8)
        PB0, PB1, PTR, PSC, PKV, POT = PBQ, PBK, PX, PSK, PSK, PX
        A = Sc.add

        xload_ops = {}

        def x_load(t):
            if t >= NT or t in xload_ops:
                return
            xs_ = xsB[t % NXB]
            aft = [xload_ops[t - 2]] if (4 <= t < NXB) else []
            if t == 4 and "w0" in xload_ops:
                aft = aft + [xload_ops["w0"]]
            xload_ops[t] = A("sp", lambda e: e.dma_start(out=xs_, in_=x[t * 128:(t + 1) * 128, :]),
                             writes=[("xsB", t % NXB)], dma_sem=d_xb[t % NXB], after=aft)

        def posi_load(b):
            A("sp", lambda e: e.dma_start(out=posi[:, b * 512:(b + 1) * 512], in_=bass.AP(pos, b * 512, [[0, 128], [1, 512]])),
              writes=[("posi", b)] + (["ropetmp"] if b == 0 else []), dma_sem=d_p[b])
        posi_load(0)
        A("sp", lambda e: e.dma_start(out=cf[:, :], in_=constf[:, :]), writes=["cf"], dma_sem=d_c[0])
        A("sp", lambda e: e.dma_start(out=vf[:, :], in_=vecs[:, :]), writes=["vf"], dma_sem=d_c[1])
        for t in range(4):
            x_load(t)
        A("pool", lambda e: e.dma_start(out=cb[:, 0:384], in_=constb[:, 0:384]), writes=["cb"], dma_sem=d_c[3])
        WB0 = bass.AP(w_in, 0, [[6 * D, 128], [128 * 6 * D, 8], [1, 512]])
        xload_ops["w0"] = A("pool", lambda e: e.dma_start(out=wbuf[0][:, :, 0:512], in_=WB0), writes=[("wbuf", 0)],
                            dma_sem=d_w[0])
        for t in range(4, NXB):
            x_load(t)
        for b in range(1, 4):
            posi_load(b)
        A("dve", lambda e: e.memset(mhalf[:, :], -0.5), writes=["mhalf"])
        A("dve", lambda e: e.memset(halfpi[:, :], math.pi / 2), writes=["halfpi"])
        A("act", lambda e: e.activation(actwarm[:, :], mhalf[:, 0:1], AF.Silu), reads=["mhalf"], writes=["actwarm"])

        def load_wblock(slot, col0, ncols, dsem, after=()):
            src = bass.AP(w_in, col0, [[6 * D, 128], [128 * 6 * D, 8], [1, ncols]])
            A("pool", lambda e: e.dma_start(out=wbuf[slot][:, :, 0:ncols], in_=src),
              writes=[("wbuf", slot)], dma_sem=dsem, after=after)

        def pb_sq(t):
            if t >= NT:
                return
            xs_ = xsB[t % NXB]
            x_load(t)
            A("act", lambda e: e.activation(sqj[:, :], xs_, AF.Square, accum_out=ssq[:, t:t + 1]),
              reads=[("xsB", t % NXB)], writes=["sqj", ("ssq", t)])
            A("pool", lambda e: e.tensor_scalar(ve_[:, t:t + 1], ssq[:, t:t + 1], 1.0 / D, EPS, ALU.mult, ALU.add),
              reads=[("ssq", t)], writes=[("ve", t)])
            A("pool", lambda e: e.tensor_tensor(rstd[:, t:t + 1], ve_[:, t:t + 1], mhalf[:, 0:1], ALU.pow),
              reads=[("ve", t), "mhalf"], writes=[("rstd", t)])

        def pb_copy(t):
            xs_ = xsB[t % NXB]; xn_ = xn[t % 4]
            if t < 4:
                if t % 2 == 0:
                    A("act", lambda e: e.activation(xn_[:, :], xs_, AF.Copy, scale=rstd[:, t:t + 1]),
                      reads=[("xsB", t % NXB), ("rstd", t)], writes=[("xn", t % 4)])
                else:
                    A("pool", lambda e: e.tensor_scalar(xn_[:, :], xs_, rstd[:, t:t + 1], 0.0, ALU.mult, ALU.add),
                      reads=[("xsB", t % NXB), ("rstd", t)], writes=[("xn", t % 4)])
            else:
                A("act", lambda e: e.activation(xn_[:, 0:512], xs_[:, 0:512], AF.Copy, scale=rstd[:, t:t + 1]),
                  reads=[("xsB", t % NXB), ("rstd", t)], writes=[("xn", t % 4, 0)])
                A("pool", lambda e: e.tensor_scalar(xn_[:, 512:1024], xs_[:, 512:1024], rstd[:, t:t + 1], 0.0, ALU.mult, ALU.add),
                  reads=[("xsB", t % NXB), ("rstd", t)], writes=[("xn", t % 4, 1)])
            x_load(t + NXB)

        def pb_tr(t):
            xn_ = xn[t % 4]; pb = PX if t % 2 == 0 else PSK

            def tr(e):
                for fc in range(8):
                    ins = e.transpose(bkb(pb)[:, fc * 128:(fc + 1) * 128], xn_[:, fc * 128:(fc + 1) * 128], ident)
                return ins
            A("pe", tr, reads=[("xn", t % 4), ("xn", t % 4, 0), ("xn", t % 4, 1), "cb"],
              writes=[("ps", pb, 0), ("ps", pb, 1), ("ps", pb)])
            A("dve", lambda e: e.tensor_tensor(
                hT[:, :, t * 128:(t + 1) * 128],
                bkb(pb).rearrange("p (a b) -> p a b", a=8),
                col_bc(gpreT, 128), ALU.mult),
              reads=[("ps", pb, 0), ("ps", pb, 1), ("ps", pb), "vf"], writes=[("hT", t)])

        def rope_dve_ops(b):
            cs = slice(b * 512, (b + 1) * 512)
            t0, t1, t2, t3, t4 = [r_[:, :] for r_ in rt]
            rd = ["ropetmp"]; wr = ["ropetmp"]
            return [
                lambda: A("dve", lambda e: e.tensor_copy(t0, posi[:, cs]), reads=[("posi", b)] + rd, writes=wr),
                lambda: A("dve", lambda e: e.tensor_scalar(t2, t0, invf, None, ALU.mult), reads=["cf"] + rd, writes=wr),
                lambda: A("dve", lambda e: e.scalar_tensor_tensor(t1, t0, invf_lo, t2, ALU.mult, ALU.add), reads=["cf"] + rd, writes=wr),
                lambda: A("dve", lambda e: e.tensor_scalar(rki[:, :], t1, 1.0 / TWO_PI, None, ALU.mult), reads=rd, writes=wr),
                lambda: A("dve", lambda e: e.tensor_copy(t0, rki[:, :]), reads=rd, writes=wr),
                lambda: A("dve", lambda e: e.scalar_tensor_tensor(t2, t0, -CW1, t1, ALU.mult, ALU.add), reads=rd, writes=wr),
                lambda: A("dve", lambda e: e.scalar_tensor_tensor(t1, t0, -CW2, t2, ALU.mult, ALU.add), reads=rd, writes=wr),
                lambda: A("dve", lambda e: e.tensor_scalar(t3, t1, PI_SAFE, -PI_SAFE, ALU.min, ALU.max), reads=rd, writes=["ropetmp", "rt3"]),
                lambda: A("dve", lambda e: e.scalar_tensor_tensor(t4, t1, -1.0, t1, ALU.mult, ALU.max), reads=rd, writes=["ropetmp", "rt4"]),
            ]

        def rope_dve(b):
            for f in rope_dve_ops(b):
                f()

        def rope_act(b):
            cs = slice(b * 512, (b + 1) * 512)
            t0, t1, t2, t3, t4 = [r_[:, :] for r_ in rt]
            A("act", lambda e: e.activation(sinT[0:64, cs], t3[0:64, :], AF.Sin, scale=-1.0),
              reads=["ropetmp", "rt3"], writes=[("sinT", b, 0)])
            A("act", lambda e: e.activation(sinT[64:128, cs], t3[64:128, :], AF.Sin),
              reads=["ropetmp", "rt3"], writes=[("sinT", b, 1)])
            A("act", lambda e: e.activation(cosT[:, cs], t4, AF.Sin, bias=halfpi[:, 0:1], scale=-1.0),
              reads=["ropetmp", "rt4", "halfpi"], writes=[("cosT", b)])

        def rope_block(b):
            rope_dve(b)
            rope_act(b)

        NU = H * NTB

        pwt = xn01[0]
        pwt2 = xn01[1]
        def pwv(g, c2, dd):
            t_ = pwt if g < 2 else pwt2
            base = ((g % 2) * 2 + c2) * 256 + dd * 128
            return t_[:, base:base + 128]

        def load_poolw():
            for gg in range(2):
                t_ = pwt if gg == 0 else pwt2
                src = bass.AP(pool_w, gg * 512 * 256, [[256, 128], [128 * 256, 4], [1, 256]])
                A("pool", lambda e, t_=t_, src=src: e.dma_start(out=t_[:, :].rearrange("p (a b) -> p a b", a=4), in_=src),
                  writes=[("pw", gg), ("xn", gg)], dma_sem=d_pw[gg])

        UCOL = 4 * D
        GCOL = 5 * D
        evq = [0]
        marks = {}

        def evac_copy(dst, src, reads, writes, after=()):
            evq[0] += 1
            if evq[0] % 2 == 0:
                A("act", lambda e: e.activation(dst, src, AF.Copy), reads=reads, writes=writes, after=after)
            else:
                A("dve", lambda e: e.tensor_copy(dst, src), reads=reads, writes=writes, after=after)

        def u_tile(t):
            pb = t % 4

            def fu(e):
                for kc in range(8):
                    ins = e.matmul(bkf(pb), hT[:, kc, t * 128:(t + 1) * 128], wbuf[0][:, kc, :],
                                   start=(kc == 0), stop=(kc == 7))
                return ins
            A("pe", fu, reads=[("wbuf", 0), ("hT", t)], writes=[("ps", pb)])
            evac_copy(u_t[t], bkf(pb), [("ps", pb)], [("u", t)], after=marks["pb"])

        def unit(i):
            return i // NTB, i % NTB

        def s1(i, part):
            h, tb = unit(i)
            cs = slice(tb * 512, (tb + 1) * 512)
            wb = wbuf[h % 2]
            hdeps = [("hT", t) for t in range(tb * 4, tb * 4 + 4)]

            def proj(bk, c0):
                def f(e):
                    for kc in range(8):
                        ins = e.matmul(bkf(bk), wb[:, kc, c0:c0 + 128], hT[:, kc, cs], start=(kc == 0), stop=(kc == 7))
                    return ins
                A("pe", f, reads=[("wbuf", h % 2)] + hdeps, writes=[("ps", bk)])
            r2 = i % 2
            nxt = [4 * (i + 1) + j for j in range(4)] if i + 1 < NTB else []

            def hook(j):
                if nxt:
                    pb_sq(nxt[j] + 2)
                    pb_copy(nxt[j])
                    ops = rope_dve_ops(i + 1)
                    for f in ops[3 * j:3 * j + 3]:
                        f()
            if part == 0:
                if 1 <= i < NTB:
                    for hf in range(2):
                        cs2 = slice(tb * 512 + hf * 256, tb * 512 + (hf + 1) * 256)

                        def fh(e, hf=hf, cs2=cs2):
                            for kc in range(8):
                                ins = e.matmul(bkf(PB0)[:, hf * 256:(hf + 1) * 256], wb[:, kc, 0:128], hT[:, kc, cs2],
                                               start=(kc == 0), stop=(kc == 7))
                            return ins
                        A("pe", fh, reads=[("wbuf", h % 2)] + [("hT", tb * 4 + 2 * hf), ("hT", tb * 4 + 2 * hf + 1)],
                          writes=[("ps", PB0)])
                else:
                    proj(PB0, 0)
                A("act", lambda e: e.activation(qraw[r2], bkf(PB0), AF.Copy), reads=[("ps", PB0)], writes=[("qraw", r2)])
                A("dve", lambda e: e.tensor_tensor(aq, bkf(PB0), cosT[:, cs], ALU.mult),
                  reads=[("ps", PB0), ("cosT", tb)], writes=["aq"])
                hook(0)
            elif part == 1:
                proj(PB1, 128)
                A("act", lambda e: e.activation(kraw[r2], bkf(PB1), AF.Copy), reads=[("ps", PB1)], writes=[("kraw", r2)])
                A("dve", lambda e: e.tensor_tensor(ak2[r2], bkf(PB1), cosT[:, cs], ALU.mult),
                  reads=[("ps", PB1), ("cosT", tb)], writes=[("ak", r2)])
                hook(1)
            elif part == 2:
                proj(PBV, 256)
                A("act", lambda e: e.activation(vTb[r2], bkf(PBV), AF.Copy), reads=[("ps", PBV)], writes=[("vTb", r2)])
                hook(2)
            else:
                proj(PBG, 384)
                sg = sgr[i % NSG]
                A("act", lambda e: e.activation(sg, bkf(PBG), AF.Silu), reads=[("ps", PBG)], writes=[("sgr", i % NSG)])
                hook(3)
                if nxt:
                    rope_act(i + 1)
                    for t_ in nxt:
                        pb_tr(t_)
                if tb == NTB - 1 and h + 2 < H:
                    load_wblock(h % 2, (h + 2) * 512, 512, d_w[h % 2])
                if tb == NTB - 1 and h == H - 2:
                    load_wblock(0, 4 * D, 512, d_w[0])
                if tb == NTB - 1 and h == H - 1:
                    load_wblock(1, 5 * D, 512, d_w[1])
                if i == 8:
                    load_poolw()
                if i == 1:
                    load_wblock(1, 512, 512, d_w[1])
                if i == 3:
                    A("pool", lambda e: e.dma_start(out=cb[:, 384:1920], in_=constb[:, 384:1920]), writes=["cbA"],
                      dma_sem=d_c[2])

        def s2q(i):
            h, tb = unit(i)
            cs = slice(tb * 512, (tb + 1) * 512)
            r2 = i % 2
            A("pe", lambda e: e.matmul(bkf(PSW), Pm, qraw[r2], start=True, stop=True),
              reads=[("qraw", r2), "cb"], writes=[("ps", PSW)])
            A("dve", lambda e: e.tensor_tensor(bq, bkf(PSW), sinT[:, cs], ALU.mult),
              reads=[("ps", PSW), ("sinT", tb, 0), ("sinT", tb, 1)], writes=["bq"])
            A("pool", lambda e: e.tensor_tensor(qT[i % NQ], aq, bq, ALU.add),
              reads=["aq", "bq"], writes=[("qT", i % NQ)])

        def s2k(i):
            h, tb = unit(i)
            cs = slice(tb * 512, (tb + 1) * 512)
            r2 = i % 2
            A("pe", lambda e: e.matmul(bkf(PSW), Pm, kraw[r2], start=True, stop=True),
              reads=[("kraw", r2), "cb"], writes=[("ps", PSW)])
            A("dve", lambda e: e.tensor_tensor(bk, bkf(PSW), sinT[:, cs], ALU.mult),
              reads=[("ps", PSW), ("sinT", tb, 0), ("sinT", tb, 1)], writes=["bk"])
            A("pool", lambda e: e.tensor_tensor(kT[i % NK], ak2[r2], bk, ALU.add),
              reads=[("ak", r2), "bk"], writes=[("kT", i % NK)])

        def s3a(i):
            h, tb = unit(i)
            r2 = i % 2
            vt = vTb[r2]; k_ = kT[i % NK]; q_ = qT[i % NQ]

            def trv(e):
                for j in range(4):
                    ins = e.transpose(bkb(PX)[:, j * 128:(j + 1) * 128], vt[:, j * 128:(j + 1) * 128], ident)
                return ins
            A("pe", trv, reads=[("vTb", r2), "cb"], writes=[("ps", PX, 0)])
            A("act", lambda e: e.activation(vsb[i % NV].rearrange("p a b -> p (a b)"), bkb(PX)[:, 0:512], AF.Copy),
              reads=[("ps", PX, 0)], writes=[("vsb", i % NV)])

            def sc(e):
                for j in range(4):
                    ins = e.matmul(bkf(PSK)[:, j * 128:(j + 1) * 128], k_[:, j * 128:(j + 1) * 128],
                                   q_[:, j * 128:(j + 1) * 128], start=True, stop=True)
                return ins
            A("pe", sc, reads=[("kT", i % NK), ("qT", i % NQ)], writes=[("ps", PSK)])
            A("dve", lambda e: e.scalar_tensor_tensor(
                SsT[r2].rearrange("p (a b) -> p a b", a=4),
                bkf(PSK).rearrange("p (a b) -> p a b", a=4),
                kscale(h), bc(mask, 4), ALU.mult, ALU.mult),
              reads=[("ps", PSK), "cf", "cb"], writes=[("SsT", r2)])

        def s3b(i):
            h, tb = unit(i)
            r2 = i % 2
            k_ = kT[i % NK]

            def trk(e):
                for j in range(4):
                    ins = e.transpose(bkb(PX)[:, 512 + j * 128:512 + (j + 1) * 128], k_[:, j * 128:(j + 1) * 128], ident)
                return ins
            A("pe", trk, reads=[("kT", i % NK), "cb"], writes=[("ps", PX, 1)])
            A("act", lambda e: e.activation(ktl[r2].rearrange("p a b -> p (a b)"), bkb(PX)[:, 512:1024], AF.Copy,
                                            scale=kscale(h)),
              reads=[("ps", PX, 1), "cf"], writes=[("kt", r2)])

        def s4(i):
            h, tb = unit(i)
            r2 = i % 2
            kt_ = ktl[r2]; v_ = vsb[i % NV]; Rb = Rbf[i % NR]

            def kv(e):
                for j in range(4):
                    ins = e.matmul(bkf(PKV)[:, j * 128:(j + 1) * 128], kt_[:, j, :], v_[:, j, :], start=True, stop=True)
                return ins
            A("pe", kv, reads=[("kt", r2), ("vsb", i % NV)], writes=[("ps", PKV)])
            for j in range(4):
                c = tb * 4 + j
                pk = bkf(PKV)[:, j * 128:(j + 1) * 128]
                Tn = Tst[h % 2][c % 2]; Tp = Tst[h % 2][(c - 1) % 2]
                if c == 0:
                    A("dve", lambda e, pk=pk, Tn=Tn: e.tensor_copy(Tn[:, :], pk),
                      reads=[("ps", PKV)], writes=[("T", h % 2, c % 2)])
                else:
                    A("dve", lambda e, pk=pk, Tn=Tn, Tp=Tp: e.scalar_tensor_tensor(Tn[:, :], Tp[:, :], g128[h], pk,
                                                                                  ALU.mult, ALU.add),
                      reads=[("ps", PKV), ("T", h % 2, (c - 1) % 2)], writes=[("T", h % 2, c % 2)])
                A("pool", lambda e, j=j, Tn=Tn: e.tensor_scalar(Rb[:, j, :], Tn[:, :], g128[h], 0.0, ALU.mult, ALU.add),
                  reads=[("T", h % 2, c % 2)], writes=[("Rbf", i % NR, j)])

        def s5a(i):
            h, tb = unit(i)
            r2 = i % 2
            v_ = vsb[i % NV]; q_ = qT[i % NQ]; S_ = SsT[r2]
            rdeps = [("SsT", r2), ("vsb", i % NV), ("qT", i % NQ)] + [("Rbf", i % NR, j) for j in range(3)]
            if tb > 0:
                rdeps.append(("Rbf", (i - 1) % NR, 3))

            def om(e):
                for j in range(4):
                    po = bkf(PO)[:, j * 128:(j + 1) * 128]
                    first = (tb == 0 and j == 0)
                    ins = e.matmul(po, S_[:, j * 128:(j + 1) * 128], v_[:, j, :], start=True, stop=first)
                    if not first:
                        Rprev = Rbf[i % NR][:, j - 1, :] if j > 0 else Rbf[(i - 1) % NR][:, 3, :]
                        ins = e.matmul(po, q_[:, j * 128:(j + 1) * 128], Rprev, start=False, stop=True)
                return ins
            A("pe", om, reads=rdeps, writes=[("ps", PO)])
            st = bst[r2]; mv = bmv[r2]
            for j in range(4):
                A("dve", lambda e, j=j: e.bn_stats(st[:, j, :], bkf(PO)[:, j * 128:(j + 1) * 128]),
                  reads=[("ps", PO)], writes=[("bst", r2, j)])
                A("dve", lambda e, j=j: e.bn_aggr(mv[:, j, :], st[:, j, :]),
                  reads=[("bst", r2, j)], writes=[("bmv", r2, j)])
            mvd = [("bmv", r2, j) for j in range(4)]
            A("dve", lambda e: e.tensor_scalar(gve[r2][:, :], mv[:, :, 1], epsc(h), None, ALU.add),
              reads=mvd + ["cf"], writes=[("gve", r2)])
            A("pool", lambda e: e.tensor_tensor(grs[r2][:, :], gve[r2][:, :], mhalf[:, 0:4], ALU.pow),
              reads=[("gve", r2), "mhalf"], writes=[("grs", r2)])

        def s5b(i):
            h, tb = unit(i)
            r2 = i % 2
            mv = bmv[r2]
            mvd = [("bmv", r2, j) for j in range(4)]
            A("dve", lambda e: e.scalar_tensor_tensor(gnb[r2][:, :], mv[:, :, 0], -1.0, grs[r2][:, :], ALU.mult, ALU.mult),
              reads=mvd + [("grs", r2)], writes=[("gnb", r2)])
            for j in range(4):
                A("act", lambda e, j=j: e.activation(onb[r2][:, j, :], bkf(PO)[:, j * 128:(j + 1) * 128], AF.Identity,
                                                     bias=gnb[r2][:, j:j + 1], scale=grs[r2][:, j:j + 1]),
                  reads=[("ps", PO), ("gnb", r2), ("grs", r2)], writes=[("on", r2, j)])

        first_y = [True]

        def s6(i):
            h, tb = unit(i)
            r2 = i % 2
            cs = slice(tb * 512, (tb + 1) * 512)
            on_ = onb[r2]

            def tro(e):
                for j in range(4):
                    ins = e.transpose(bkb(POT)[:, j * 128:(j + 1) * 128], on_[:, j, :], ident)
                return ins
            A("pe", tro, reads=[("on", r2, j) for j in range(4)] + ["cb"], writes=[("ps", POT, 0)])
            wr = [("yT", h, tb)]
            if first_y[0]:
                wr += ["ropetmp"] + [("posi", b) for b in range(4)]
                first_y[0] = False
            A("dve", lambda e: e.scalar_tensor_tensor(yT[:, h, cs], bkb(POT)[:, 0:512], normg(h), sgr[i % NSG],
                                                      ALU.mult, ALU.mult),
              reads=[("ps", POT, 0), "vf", ("sgr", i % NSG)], writes=wr)

        for i in range(NU + 5):
            if i == 0:
                rope_dve(0)
                pb_sq(0)
                pb_sq(1)
                for t in range(4):
                    pb_copy(t)
                    if t + 2 < 4:
                        pb_sq(t + 2)
                    pb_tr(t)
                pb_sq(4)
                pb_sq(5)
                rope_act(0)
            ok = lambda u: 0 <= u < NU
            if ok(i - 1): s2q(i - 1)
            if ok(i - 5): s6(i - 5)
            if ok(i - 4): s5a(i - 4)
            dr = i - NU
            def big(part):
                if i < NU:
                    s1(i, part)
                elif dr < 4:
                    u_tile(4 * dr + part)
            big(0)
            if ok(i - 2): s3a(i - 2)
            big(1)
            if ok(i - 2): s3b(i - 2)
            if ok(i - 3): s4(i - 3)
            big(2)
            if ok(i - 4): s5b(i - 4)
            if ok(i - 1): s2k(i - 1)
            big(3)
            if i == NTB - 1:
                marks["pb"] = Sc.mark()

        ret_done = Sc.mark()

        for q4 in range(4):
            src = bass.AP(w_out, q4 * 512 * D, [[D, 128], [128 * D, 4], [1, D]])
            A("pool", lambda e, q4=q4, src=src: e.dma_start(out=wsb[:, q4 * 4:(q4 + 1) * 4, :], in_=src),
              writes=[("wsb", q4)], dma_sem=d_wo, after=ret_done)

        for half in range(2):
            if half == 1:
                load_wblock(0, UCOL + half * 512, 512, d_w[0])
                load_wblock(1, GCOL + half * 512, 512, d_w[1])
            if half == 1:
                for t in range(NT):
                    u_tile(t)
            for tb in range(NTB):
                for cc in range(4):
                    g = 2 * half + cc // 2
                    pbk = (tb * 4 + cc) % 4
                    def fm(e, tb=tb, cc=cc, g=g, pbk=pbk):
                        for tt in range(4):
                            t = tb * 4 + tt
                            o_ = bkf(pbk)[:, tt * 128:(tt + 1) * 128]
                            ins = e.matmul(o_, u_t[t][:, cc * 128:(cc + 1) * 128], Amat(g, 0 if t == 0 else 1),
                                           start=True, stop=(t == 0))
                            if t > 0:
                                ins = e.matmul(o_, u_t[t - 1][:, cc * 128:(cc + 1) * 128], Amat(g, 2),
                                               start=False, stop=True)
                        return ins
                    rd = [("u", t) for t in range(max(0, tb * 4 - 1), tb * 4 + 4)] + ["cbA"]
                    A("pe", fm, reads=rd, writes=[("ps", pbk)])
                    evac_copy(mixT[:, cc, tb * 512:(tb + 1) * 512], bkf(pbk), [("ps", pbk)], [("mix", cc, tb)], after=ret_done)
            for f in range(4):
                g = 2 * half + f // 2
                dd = f % 2
                for tb in range(NTB):
                    cs = slice(tb * 512, (tb + 1) * 512)
                    n_ = (f * 4 + tb)
                    pg = 4 + n_ % 2
                    pp = 6 + n_ % 2
                    def fg(e, f=f, cs=cs, pg=pg):
                        for kc in range(8):
                            ins = e.matmul(bkf(pg), wbuf[1][:, kc, f * 128:(f + 1) * 128], hT[:, kc, cs],
                                           start=(kc == 0), stop=(kc == 7))
                        return ins
                    A("pe", fg, reads=[("wbuf", 1)] + [("hT", t) for t in range(tb * 4, tb * 4 + 4)],
                      writes=[("ps", pg)])
                    sg = sgp[n_ % 3]
                    A("act", lambda e, sg=sg, pg=pg: e.activation(sg, bkf(pg), AF.Silu),
                      reads=[("ps", pg)], writes=[("sgp", n_ % 3)])
                    def fp(e, g=g, dd=dd, f=f, cs=cs, pp=pp):
                        for c2 in range(2):
                            ins = e.matmul(bkf(pp), pwv(g, c2, dd), mixT[:, 2 * (f // 2) + c2, cs],
                                           start=(c2 == 0), stop=(c2 == 1))
                        return ins
                    A("pe", fp, reads=[("pw", g // 2), ("mix", 2 * (f // 2), tb), ("mix", 2 * (f // 2) + 1, tb)],
                      writes=[("ps", pp)])
                    ych = 8 + half * 4 + f
                    A("dve", lambda e, ych=ych, cs=cs, pp=pp, sg=sg: e.scalar_tensor_tensor(
                        yT[:, ych, cs], bkf(pp), pscale(ych - 8), sg, ALU.mult, ALU.mult),
                      reads=[("ps", pp), "vf", ("sgp", n_ % 3)], writes=[("yT", ych, tb)], after=marks["pb"])

        Sc.barrier()

        A("sp", lambda e: e.dma_start(out=gpb, in_=bass.AP(gpost, 0, [[0, 128], [1, D]])), writes=["gpb"], dma_sem=d_c[4])
        for t in range(NT):
            xs_ = xst[t % 3]
            A("sp", lambda e, t=t, xs_=xs_: e.dma_start(out=xs_[:, :], in_=x[t * 128:(t + 1) * 128, :]),
              writes=[("xst", t % 3)], dma_sem=d_x[t % 3])
            b0 = (t % 2) * 2
            def fo(e, t=t, b0=b0):
                for n in range(2):
                    for fch in range(16):
                        ins = e.matmul(bkf(b0 + n), yT[:, fch, t * 128:(t + 1) * 128], wsb[:, fch, n * 512:(n + 1) * 512],
                                       start=(fch == 0), stop=(fch == 15))
                return ins
            A("pe", fo, reads=[("wsb", q4) for q4 in range(4)] + [("yT", c, t // 4) for c in range(16)],
              writes=[("ps", b0), ("ps", b0 + 1)])
            r2 = t % 2
            for n in range(2):
                A("act", lambda e, n=n, b0=b0, r2=r2: e.activation(ftmp[r2][:, n * 512:(n + 1) * 512], bkf(b0 + n), AF.Square,
                                                                   accum_out=fss[r2][:, n:n + 1]),
                  reads=[("ps", b0 + n)], writes=[("ftmp", r2, n), ("fss", r2, n)])
            A("dve", lambda e, r2=r2: e.tensor_tensor(fss[r2][:, 2:3], fss[r2][:, 0:1], fss[r2][:, 1:2], ALU.add),
              reads=[("fss", r2, 0), ("fss", r2, 1)], writes=[("fss", r2, 2)])
            A("dve", lambda e, r2=r2: e.tensor_scalar(fss[r2][:, 3:4], fss[r2][:, 2:3], 1.0 / D, EPS, ALU.mult, ALU.add),
              reads=[("fss", r2, 2)], writes=[("fss", r2, 3)])
            A("pool", lambda e, r2=r2: e.tensor_tensor(frs[r2][:, 0:1], fss[r2][:, 3:4], mhalf[:, 0:1], ALU.pow),
              reads=[("fss", r2, 3), "mhalf"], writes=[("frs", r2)])
            for n in range(2):
                hs = slice(n * 512, (n + 1) * 512)
                A("dve", lambda e, n=n, b0=b0, r2=r2, hs=hs: e.scalar_tensor_tensor(
                    ftmp[r2][:, hs], bkf(b0 + n), frs[r2][:, 0:1], gpb[:, hs],
                    ALU.mult, ALU.mult),
                  reads=[("ps", b0 + n), ("frs", r2), "gpb"], writes=[("ftmp", r2, n)])
                r3 = t % 3
                A("dve", lambda e, r2=r2, r3=r3, xs_=xs_, hs=hs: e.tensor_tensor(fout[r3][:, hs], ftmp[r2][:, hs], xs_[:, hs], ALU.add),
                  reads=[("ftmp", r2, n), ("xst", t % 3)], writes=[("fout", r3, n)])
                A("sp", lambda e, t=t, r3=r3, hs=hs: e.dma_start(out=out[t * 128:(t + 1) * 128, hs], in_=fout[r3][:, hs]),
                  reads=[("fout", r3, n)], dma_sem=d_o[r3][n])

        Sc.emit(sems, final_waits=[(d_o[a][b], 16 * len([t for t in range(NT) if t % 3 == a])) for a in range(3) for b in range(2)])
    return nc


def _consts():
    gam = 1.0 - 2.0 ** (-5.0 - np.arange(H, dtype=np.float64))
    s = np.arange(128, dtype=np.float64)
    cf = np.zeros((128, 24), np.float64)
    cf[:, 0:8] = gam[None, :] ** (-(s[:, None] + 1.0)) * (128.0 ** -0.5)
    cf[:, 8:16] = EPS * gam[None, :] ** (-2.0 * (s[:, None] + 1.0))
    half = 64
    inv_freq = 10000.0 ** (-np.arange(half, dtype=np.float64) / half)
    f_hi = inv_freq.astype(np.float32).astype(np.float64)
    f_lo = inv_freq - f_hi
    cf[:, 16] = np.concatenate([f_hi, f_hi])
    cf[:, 17] = np.concatenate([f_lo, f_lo])
    cb = np.zeros((128, 15 * 128), np.float64)
    cb[:, 0:128] = np.eye(128)
    P = np.zeros((128, 128))
    for m in range(128):
        P[(m + 64) % 128, m] = 1.0
    cb[:, 128:256] = P
    ss, cc = np.meshgrid(np.arange(128), np.arange(128), indexing="ij")
    cb[:, 256:384] = (cc >= ss).astype(np.float64)
    for g, w in enumerate(WINDOWS):
        t = cc; s_ = ss
        cnt = np.minimum(t + 1, w).astype(np.float64)
        a0 = ((s_ <= t) & (s_ > t - w)) / cnt - (s_ == t)
        a1 = ((s_ <= t) & (s_ > t - w)) / float(w) - (s_ == t)
        a2 = ((s_ - 128) > (t - w)) / float(w)
        cb[:, (3 + 3 * g + 0) * 128:(3 + 3 * g + 1) * 128] = a0
        cb[:, (3 + 3 * g + 1) * 128:(3 + 3 * g + 2) * 128] = a1
        cb[:, (3 + 3 * g + 2) * 128:(3 + 3 * g + 3) * 128] = a2
    return cf.astype(np.float32), cb.astype(np.float32)


_PROGRAM = None


def kernel(x, positions, w_in, w_out, pool_w, pool_scale, ret_norm_g, pre_norm_g, post_norm_g):
    global _PROGRAM
    x = np.asarray(x); positions = np.asarray(positions)
    w_in = np.asarray(w_in)[0]; w_out = np.asarray(w_out)[0]; pool_w = np.asarray(pool_w)[0]
    B = x.shape[0]
    assert B == 8 and x.shape[1] == S and x.shape[2] == D
    blocks = []
    for h in range(H):
        for p in (0, 1, 2, 3):
            blocks.append(w_in[:, p * D + h * 128: p * D + (h + 1) * 128])
    blocks.append(w_in[:, 4 * D:5 * D])
    blocks.append(w_in[:, 5 * D:6 * D])
    w_in_r = np.ascontiguousarray(np.concatenate(blocks, axis=1), dtype=np.float32)
    vecs = np.zeros((128, 24), np.float32)
    vecs[:, 0:8] = np.asarray(pre_norm_g)[0].reshape(8, 128).T
    vecs[:, 8:16] = np.asarray(ret_norm_g)[0].reshape(8, 128).T
    vecs[:, 16:24] = np.asarray(pool_scale)[0].reshape(8, 128).T
    gpost = np.ascontiguousarray(np.asarray(post_norm_g)[0].reshape(1, D), dtype=np.float32)
    cf, cb = _consts()
    if _PROGRAM is None:
        _PROGRAM = build_program()
    nc = _PROGRAM
    in_maps = []
    for b in range(B):
        in_maps.append({
            "x": np.ascontiguousarray(x[b], dtype=np.float32),
            "pos": np.ascontiguousarray(positions[b].reshape(1, S), dtype=np.int32),
            "w_in": w_in_r,
            "w_out": np.ascontiguousarray(w_out, dtype=np.float32),
            "pool_w": np.ascontiguousarray(pool_w.reshape(4 * 256, 256), dtype=np.float32),
            "vecs": vecs, "gpost": gpost, "constf": cf, "constb": cb,
        })
    res = run_bass_kernel_spmd(nc, in_maps, core_ids=list(range(B)))
    return np.stack([np.asarray(r["out"]).reshape(S, D) for r in res.results], axis=0).astype(np.float32)


if __name__ == "__main__":
    import time
    t0 = time.time()
    nc = build_program()
    print("built in", time.time() - t0)
```

```python
import math
from contextlib import ExitStack

import numpy as np
import concourse.bass as bass
import concourse.mybir as mybir
from concourse.bass_utils import run_bass_kernel_spmd

F32 = mybir.dt.float32
BF16 = mybir.dt.bfloat16
I32 = mybir.dt.int32
AF = mybir.ActivationFunctionType
ALU = mybir.AluOpType

D = 1024
S = 2048
H = 8
NT = 16
NTB = 4
EPS = 1e-6
WINDOWS = (2, 4, 8, 16)
TWO_PI = 2.0 * math.pi
CW1 = 6.28125
CW2 = TWO_PI - CW1
PI_SAFE = 3.1415925


class _Op:
    __slots__ = ("eng", "fn", "deps", "is_dma", "dma_sem", "dma_val", "signal", "count", "name", "idx", "waits", "know")


class Sched:
    ENGS = ("pe", "act", "dve", "pool", "sp")

    def __init__(self, nc):
        self.nc = nc
        self.ops = []
        self.last_writer = {}
        self.readers = {}
        self.dma_sem_count = {}
        self.barrier_deps = []
        self.last_of_eng = {}
        self.last_of_dsem = {}
        self.excl_last = {}

    def add(self, eng, fn, reads=(), writes=(), dma_sem=None, name="", after=()):
        op = _Op()
        op.eng = eng; op.fn = fn; op.name = name
        op.is_dma = dma_sem is not None
        op.dma_sem = dma_sem
        op.signal = False; op.count = 0; op.dma_val = 0
        op.idx = len(self.ops)
        deps = set(self.barrier_deps)
        deps.update(after)
        for r in reads:
            w = self.last_writer.get(r)
            if w is not None:
                deps.add(w)
        for w_ in writes:
            lw = self.last_writer.get(w_)
            if lw is not None:
                deps.add(lw)
            for rd in self.readers.get(w_, ()):
                deps.add(rd)
        banks = set()
        for r in list(reads) + list(writes):
            if isinstance(r, tuple) and r[0] == "ps":
                banks.add(r[1])
        for b in banks:
            g = self.excl_last.get(b)
            if g is not None and g[0] != eng:
                deps.add(g[1])
            self.excl_last[b] = (eng, op)
        deps.discard(op)
        op.deps = deps
        for r in reads:
            self.readers.setdefault(r, []).append(op)
        for w_ in writes:
            self.last_writer[w_] = op
            self.readers[w_] = []
        if op.is_dma:
            c = self.dma_sem_count.get(id(dma_sem), 0) + 16
            self.dma_sem_count[id(dma_sem)] = c
            op.dma_val = c
            self.last_of_dsem[id(dma_sem)] = op
        else:
            self.last_of_eng[eng] = op
        self.ops.append(op)
        return op

    def mark(self):
        return list(self.last_of_eng.values()) + list(self.last_of_dsem.values())

    def barrier(self):
        self.barrier_deps = list(self.last_of_eng.values()) + list(self.last_of_dsem.values())

    def emit(self, sems, final_waits=()):
        nc = self.nc
        for op in self.ops:
            for d in op.deps:
                if d.is_dma:
                    continue
                if d.eng == "pe" and op.eng == "pe" and not op.is_dma:
                    continue
                d.signal = True
        counts = {e: 0 for e in self.ENGS}
        for op in self.ops:
            if op.is_dma:
                continue
            if op.signal:
                counts[op.eng] += 1
                op.count = counts[op.eng]
        per_eng = {e: [o for o in self.ops if o.eng == e] for e in self.ENGS}

        eng_know = {e: {} for e in self.ENGS}
        self.n_waits = 0
        self.n_skipped = 0
        for op in self.ops:
            need = {}
            for d in op.deps:
                if d.is_dma:
                    key = ("dma", id(d.dma_sem)); sem = d.dma_sem; val = d.dma_val
                else:
                    if d.eng == "pe" and op.eng == "pe" and not op.is_dma:
                        continue
                    key = d.eng; sem = sems[d.eng]; val = d.count
                if val > need.get(key, (None, 0, None))[1]:
                    need[key] = (sem, val, d)
            know = eng_know[op.eng]
            waits = []
            for key, (sem, val, d) in sorted(need.items(), key=lambda kv: -kv[1][1]):
                if know.get(key, 0) >= val:
                    self.n_skipped += 1
                    continue
                waits.append((sem, val))
                self.n_waits += 1
                know[key] = val
                for k2, v2 in d.know.items():
                    if v2 > know.get(k2, 0):
                        know[k2] = v2
            op.waits = waits
            op.know = dict(know)
            if op.is_dma:
                op.know[("dma", id(op.dma_sem))] = op.dma_val
            elif op.signal:
                op.know[op.eng] = max(op.know.get(op.eng, 0), op.count)

        def run_stream(e, engobj):
            for op in per_eng[e]:
                for sem, val in op.waits:
                    engobj.wait_ge(sem, val)
                ins = op.fn(engobj)
                if op.is_dma:
                    ins.then_inc(op.dma_sem, 16)
                elif op.signal:
                    ins.then_inc(sems[op.eng], 1)
            if e == "sp":
                for sem, val in final_waits:
                    engobj.wait_ge(sem, val)

        with nc.Block() as block:
            @block.tensor
            def _(eng):
                run_stream("pe", eng)

            @block.scalar
            def _(eng):
                run_stream("act", eng)

            @block.vector
            def _(eng):
                run_stream("dve", eng)

            @block.gpsimd
            def _(eng):
                run_stream("pool", eng)

            @block.sync
            def _(eng):
                run_stream("sp", eng)


def bc(ap2d, n_mid):
    (ps, pn), (st, n) = ap2d.ap
    return bass.AP(ap2d.tensor, ap2d.offset, [[ps, pn], [0, n_mid], [st, n]])


def col_bc(ap2d, n_in):
    (ps, pn), (st, k) = ap2d.ap
    return bass.AP(ap2d.tensor, ap2d.offset, [[ps, pn], [st, k], [0, n_in]])


def build_program():
    nc = bass.Bass("TRN2", target_bir_lowering=False)
    x = nc.dram_tensor("x", [S, D], F32, kind="ExternalInput")
    pos = nc.dram_tensor("pos", [1, S], I32, kind="ExternalInput")
    w_in = nc.dram_tensor("w_in", [D, 6 * D], F32, kind="ExternalInput")
    w_out = nc.dram_tensor("w_out", [2 * D, D], F32, kind="ExternalInput")
    pool_w = nc.dram_tensor("pool_w", [4 * 256, 256], F32, kind="ExternalInput")
    vecs = nc.dram_tensor("vecs", [128, 24], F32, kind="ExternalInput")
    gpost = nc.dram_tensor("gpost", [1, D], F32, kind="ExternalInput")
    constf = nc.dram_tensor("constf", [128, 24], F32, kind="ExternalInput")
    constb = nc.dram_tensor("constb", [128, 15 * 128], F32, kind="ExternalInput")
    out = nc.dram_tensor("out", [S, D], F32, kind="ExternalOutput")

    gam = [1.0 - 2.0 ** (-5.0 - h) for h in range(H)]
    g128 = [g ** 128 for g in gam]

    with ExitStack() as ctx:
        def sb(name, shape, dt):
            return ctx.enter_context(nc.sbuf_tensor(name, shape, dt))

        def sem(name):
            return ctx.enter_context(nc.semaphore(name))

        hT = sb("hT", [128, 8, S], BF16)
        yT = sb("yT", [128, 16, S], BF16)
        RW = sb("RW", [128, 16384], BF16)
        RP = sb("RP", [128, 18432], BF16)
        wbuf = [sb(f"wbuf{i}", [128, 8, 512], BF16) for i in range(2)]
        xst = [sb(f"xst{i}", [128, D], F32) for i in range(3)]
        xn01 = [sb(f"xn{i}", [128, D], BF16) for i in range(2)]
        cb = sb("cb", [128, 15 * 128], BF16)
        cf = sb("cf", [128, 24], F32)
        vf = sb("vf", [128, 24], F32)
        Tst = [[sb(f"Tst{i}_{k}", [128, 128], F32) for k in range(2)] for i in range(2)]
        ssq = sb("ssq", [128, 16], F32)
        rstd = sb("rstd", [128, 16], F32)
        ve_ = sb("ve", [128, 16], F32)
        mhalf = sb("mhalf", [128, 16], F32)
        halfpi = sb("halfpi", [128, 1], F32)
        actwarm = sb("actwarm", [128, 1], F32)
        bst = [sb(f"bst{i}", [128, 4, 6], F32) for i in range(2)]
        bmv = [sb(f"bmv{i}", [128, 4, 2], F32) for i in range(2)]
        gve = [sb(f"gve{i}", [128, 4], F32) for i in range(2)]
        grs = [sb(f"grs{i}", [128, 4], F32) for i in range(2)]
        gnb = [sb(f"gnb{i}", [128, 4], F32) for i in range(2)]
        fss = [sb(f"fss{i}", [128, 4], F32) for i in range(2)]
        frs = [sb(f"frs{i}", [128, 2], F32) for i in range(2)]

        cosT = RW[:, 0:4096].bitcast(F32)
        sinT = RW[:, 4096:8192].bitcast(F32)
        aq = RW[:, 8192:9216].bitcast(F32)
        bq = RW[:, 9216:10240].bitcast(F32)
        ak2 = [RW[:, 10240:11264].bitcast(F32), RW[:, 11264:12288].bitcast(F32)]
        bk = RW[:, 12288:13312].bitcast(F32)
        NQ, NK = 4, 2
        qT = [RW[:, 13312 + 512 * i:13312 + 512 * (i + 1)] for i in range(NQ)]
        kT = [RW[:, 15360 + 512 * i:15360 + 512 * (i + 1)] for i in range(NK)]
        wsb = RW[:, :].rearrange("p (a b) -> p a b", a=16)

        def rp(off, n):
            return RP[:, off:off + n]
        o = 0
        qraw = [rp(o + 512 * i, 512) for i in range(2)]; o += 1024
        kraw = [rp(o + 512 * i, 512) for i in range(2)]; o += 1024
        vTb = [rp(o + 512 * i, 512) for i in range(2)]; o += 1024
        NSG = 7
        sgr = [rp(o + 512 * i, 512) for i in range(NSG)]; o += 512 * NSG
        NV = 4
        vsb = [rp(o + 512 * i, 512).rearrange("p (a b) -> p a b", a=4) for i in range(NV)]; o += 512 * NV
        ktl = [rp(o + 512 * i, 512).rearrange("p (a b) -> p a b", a=4) for i in range(2)]; o += 1024
        SsT = [rp(o + 512 * i, 512) for i in range(2)]; o += 1024
        NR = 3
        Rbf = [rp(o + 512 * i, 512).rearrange("p (a b) -> p a b", a=4) for i in range(NR)]; o += 512 * NR
        onb = [rp(o + 512 * i, 512).rearrange("p (a b) -> p a b", a=4) for i in range(2)]; o += 1024
        assert o <= 13312, o
        sqj = RP[:, 13312:14336]
        xn = [xn01[0][:, :], xn01[1][:, :], RP[:, 14336:15360], RP[:, 15360:16384]]
        u_t = []
        for k in range(3):
            xb_ = xst[k][:, :].bitcast(BF16)
            u_t += [xb_[:, 512 * j:512 * (j + 1)] for j in range(4)]
        u_t += [RP[:, 13312 + 512 * j:13312 + 512 * (j + 1)] for j in range(4)]
        mixT = RP[:, 0:8192].rearrange("p (a b) -> p a b", a=4)
        sgp = [RP[:, 15360 + 512 * i:15360 + 512 * (i + 1)] for i in range(3)]
        ftmp = [RP[:, 2048 * i:2048 * (i + 1)].bitcast(F32) for i in range(2)]
        fout = [RP[:, 4096 + 2048 * i:4096 + 2048 * (i + 1)].bitcast(F32) for i in range(3)]
        gpb = RP[:, 10240:12288].bitcast(F32)
        NXB = 8
        xsB = [yT[:, 8 + k, :].bitcast(F32) for k in range(NXB)]
        yflat = yT[:, :, :].rearrange("p a b -> p (a b)")
        posi = yflat[:, 0:4096].bitcast(I32)
        rt = [yflat[:, 4096 + 1024 * i:4096 + 1024 * (i + 1)].bitcast(F32) for i in range(5)]
        rki = yflat[:, 4096 + 5120:4096 + 6144].bitcast(I32)

        ident = cb[:, 0:128]
        Pm = cb[:, 128:256]
        mask = cb[:, 256:384]
        def Amat(g, k):
            c0 = (3 + 3 * g + k) * 128
            return cb[:, c0:c0 + 128]
        kscale = lambda h: cf[:, h:h + 1]
        epsc = lambda h: cf[:, 8 + h:9 + h]
        invf = cf[:, 16:17]
        invf_lo = cf[:, 17:18]
        gpreT = vf[:, 0:8]
        normg = lambda h: vf[:, 8 + h:9 + h]
        pscale = lambda c: vf[:, 16 + c:17 + c]

        bank = [ctx.enter_context(nc.psum_tensor(f"bank{i}", [128, 512], F32)) for i in range(8)]
        def bkf(i):
            return bank[i][:, :]
        def bkb(i):
            return bank[i][:, :].bitcast(BF16)

        sems = {e: sem("s_" + e) for e in ("pe", "act", "dve", "pool")}
        d_c = [sem(f"d_c{i}") for i in range(5)]
        d_p = [sem(f"d_p{i}") for i in range(4)]
        d_x = [sem(f"d_x{i}") for i in range(3)]
        d_xb = [sem(f"d_xb{i}") for i in range(8)]
        d_w = [sem(f"d_w{i}") for i in range(2)]
        d_wo = sem("d_wo")
        d_pw = [sem(f"d_pw{i}") for i in range(2)]
        d_o = [[sem(f"d_o{i}_{n}") for n in range(2)] for i in range(3)]

        Sc = Sched(nc)
        PBQ, PBK, PBV, PBG, PSW, PX, PSK, PO = range(8)
        PB0, PB1, PTR, PSC, PKV, POT = PBQ, PBK, PX, PSK, PSK, PX
        A = Sc.add

        xload_ops = {}

        def x_load(t):
            if t >= NT or t in xload_ops:
                return
            xs_ = xsB[t % NXB]
            aft = [xload_ops[t - 2]] if (4 <= t < NXB) else []
            if t == 4 and "w0" in xload_ops:
                aft = aft + [xload_ops["w0"]]
            xload_ops[t] = A("sp", lambda e: e.dma_start(out=xs_, in_=x[t * 128:(t + 1) * 128, :]),
                             writes=[("xsB", t % NXB)], dma_sem=d_xb[t % NXB], after=aft)

        def posi_load(b):
            A("sp", lambda e: e.dma_start(out=posi[:, b * 512:(b + 1) * 512], in_=bass.AP(pos, b * 512, [[0, 128], [1, 512]])),
              writes=[("posi", b)] + (["ropetmp"] if b == 0 else []), dma_sem=d_p[b])
        posi_load(0)
        A("sp", lambda e: e.dma_start(out=cf[:, :], in_=constf[:, :]), writes=["cf"], dma_sem=d_c[0])
        A("sp", lambda e: e.dma_start(out=vf[:, :], in_=vecs[:, :]), writes=["vf"], dma_sem=d_c[1])
        for t in range(4):
            x_load(t)
        A("pool", lambda e: e.dma_start(out=cb[:, 0:384], in_=constb[:, 0:384]), writes=["cb"], dma_sem=d_c[3])
        WB0 = bass.AP(w_in, 0, [[6 * D, 128], [128 * 6 * D, 8], [1, 512]])
        xload_ops["w0"] = A("pool", lambda e: e.dma_start(out=wbuf[0][:, :, 0:512], in_=WB0), writes=[("wbuf", 0)],
                            dma_sem=d_w[0])
        for t in range(4, NXB):
            x_load(t)
        for b in range(1, 4):
            posi_load(b)
        A("dve", lambda e: e.memset(mhalf[:, :], -0.5), writes=["mhalf"])
        A("dve", lambda e: e.memset(halfpi[:, :], math.pi / 2), writes=["halfpi"])
        A("act", lambda e: e.activation(actwarm[:, :], mhalf[:, 0:1], AF.Silu), reads=["mhalf"], writes=["actwarm"])

        def load_wblock(slot, col0, ncols, dsem, after=()):
            src = bass.AP(w_in, col0, [[6 * D, 128], [128 * 6 * D, 8], [1, ncols]])
            A("pool", lambda e: e.dma_start(out=wbuf[slot][:, :, 0:ncols], in_=src),
              writes=[("wbuf", slot)], dma_sem=dsem, after=after)

        def pb_sq(t):
            if t >= NT:
                return
            xs_ = xsB[t % NXB]
            x_load(t)
            A("act", lambda e: e.activation(sqj[:, :], xs_, AF.Square, accum_out=ssq[:, t:t + 1]),
              reads=[("xsB", t % NXB)], writes=["sqj", ("ssq", t)])
            A("pool", lambda e: e.tensor_scalar(ve_[:, t:t + 1], ssq[:, t:t + 1], 1.0 / D, EPS, ALU.mult, ALU.add),
              reads=[("ssq", t)], writes=[("ve", t)])
            A("pool", lambda e: e.tensor_tensor(rstd[:, t:t + 1], ve_[:, t:t + 1], mhalf[:, 0:1], ALU.pow),
              reads=[("ve", t), "mhalf"], writes=[("rstd", t)])

        def pb_copy(t):
            xs_ = xsB[t % NXB]; xn_ = xn[t % 4]
            if t < 4:
                if t % 2 == 0:
                    A("act", lambda e: e.activation(xn_[:, :], xs_, AF.Copy, scale=rstd[:, t:t + 1]),
                      reads=[("xsB", t % NXB), ("rstd", t)], writes=[("xn", t % 4)])
                else:
                    A("pool", lambda e: e.tensor_scalar(xn_[:, :], xs_, rstd[:, t:t + 1], 0.0, ALU.mult, ALU.add),
                      reads=[("xsB", t % NXB), ("rstd", t)], writes=[("xn", t % 4)])
            else:
                A("act", lambda e: e.activation(xn_[:, 0:512], xs_[:, 0:512], AF.Copy, scale=rstd[:, t:t + 1]),
                  reads=[("xsB", t % NXB), ("rstd", t)], writes=[("xn", t % 4, 0)])
                A("pool", lambda e: e.tensor_scalar(xn_[:, 512:1024], xs_[:, 512:1024], rstd[:, t:t + 1], 0.0, ALU.mult, ALU.add),
                  reads=[("xsB", t % NXB), ("rstd", t)], writes=[("xn", t % 4, 1)])
            x_load(t + NXB)

        def pb_tr(t):
            xn_ = xn[t % 4]; pb = PX if t % 2 == 0 else PSK

            def tr(e):
                for fc in range(8):
                    ins = e.transpose(bkb(pb)[:, fc * 128:(fc + 1) * 128], xn_[:, fc * 128:(fc + 1) * 128], ident)
                return ins
            A("pe", tr, reads=[("xn", t % 4), ("xn", t % 4, 0), ("xn", t % 4, 1), "cb"],
              writes=[("ps", pb, 0), ("ps", pb, 1), ("ps", pb)])
            A("dve", lambda e: e.tensor_tensor(
                hT[:, :, t * 128:(t + 1) * 128],
                bkb(pb).rearrange("p (a b) -> p a b", a=8),
                col_bc(gpreT, 128), ALU.mult),
              reads=[("ps", pb, 0), ("ps", pb, 1), ("ps", pb), "vf"], writes=[("hT", t)])

        def rope_dve_ops(b):
            cs = slice(b * 512, (b + 1) * 512)
            t0, t1, t2, t3, t4 = [r_[:, :] for r_ in rt]
            rd = ["ropetmp"]; wr = ["ropetmp"]
            return [
                lambda: A("dve", lambda e: e.tensor_copy(t0, posi[:, cs]), reads=[("posi", b)] + rd, writes=wr),
                lambda: A("dve", lambda e: e.tensor_scalar(t2, t0, invf, None, ALU.mult), reads=["cf"] + rd, writes=wr),
                lambda: A("dve", lambda e: e.scalar_tensor_tensor(t1, t0, invf_lo, t2, ALU.mult, ALU.add), reads=["cf"] + rd, writes=wr),
                lambda: A("dve", lambda e: e.tensor_scalar(rki[:, :], t1, 1.0 / TWO_PI, None, ALU.mult), reads=rd, writes=wr),
                lambda: A("dve", lambda e: e.tensor_copy(t0, rki[:, :]), reads=rd, writes=wr),
                lambda: A("dve", lambda e: e.scalar_tensor_tensor(t2, t0, -CW1, t1, ALU.mult, ALU.add), reads=rd, writes=wr),
                lambda: A("dve", lambda e: e.scalar_tensor_tensor(t1, t0, -CW2, t2, ALU.mult, ALU.add), reads=rd, writes=wr),
                lambda: A("dve", lambda e: e.tensor_scalar(t3, t1, PI_SAFE, -PI_SAFE, ALU.min, ALU.max), reads=rd, writes=["ropetmp", "rt3"]),
                lambda: A("dve", lambda e: e.scalar_tensor_tensor(t4, t1, -1.0, t1, ALU.mult, ALU.max), reads=rd, writes=["ropetmp", "rt4"]),
            ]

        def rope_dve(b):
            for f in rope_dve_ops(b):
                f()

        def rope_act(b):
            cs = slice(b * 512, (b + 1) * 512)
            t0, t1, t2, t3, t4 = [r_[:, :] for r_ in rt]
            A("act", lambda e: e.activation(sinT[0:64, cs], t3[0:64, :], AF.Sin, scale=-1.0),
              reads=["ropetmp", "rt3"], writes=[("sinT", b, 0)])
            A("act", lambda e: e.activation(sinT[64:128, cs], t3[64:128, :], AF.Sin),
              reads=["ropetmp", "rt3"], writes=[("sinT", b, 1)])
            A("act", lambda e: e.activation(cosT[:, cs], t4, AF.Sin, bias=halfpi[:, 0:1], scale=-1.0),
              reads=["ropetmp", "rt4", "halfpi"], writes=[("cosT", b)])

        def rope_block(b):
            rope_dve(b)
            rope_act(b)

        NU = H * NTB

        pwt = xn01[0]
        pwt2 = xn01[1]
        def pwv(g, c2, dd):
            t_ = pwt if g < 2 else pwt2
            base = ((g % 2) * 2 + c2) * 256 + dd * 128
            return t_[:, base:base + 128]

        def load_poolw():
            for gg in range(2):
                t_ = pwt if gg == 0 else pwt2
                src = bass.AP(pool_w, gg * 512 * 256, [[256, 128], [128 * 256, 4], [1, 256]])
                A("pool", lambda e, t_=t_, src=src: e.dma_start(out=t_[:, :].rearrange("p (a b) -> p a b", a=4), in_=src),
                  writes=[("pw", gg), ("xn", gg)], dma_sem=d_pw[gg])

        UCOL = 4 * D
        GCOL = 5 * D
        evq = [0]
        marks = {}

        def evac_copy(dst, src, reads, writes, after=()):
            evq[0] += 1
            if evq[0] % 2 == 0:
                A("act", lambda e: e.activation(dst, src, AF.Copy), reads=reads, writes=writes, after=after)
            else:
                A("dve", lambda e: e.tensor_copy(dst, src), reads=reads, writes=writes, after=after)

        def u_tile(t):
            pb = t % 4

            def fu(e):
                for kc in range(8):
                    ins = e.matmul(bkf(pb), hT[:, kc, t * 128:(t + 1) * 128], wbuf[0][:, kc, :],
                                   start=(kc == 0), stop=(kc == 7))
                return ins
            A("pe", fu, reads=[("wbuf", 0), ("hT", t)], writes=[("ps", pb)])
            evac_copy(u_t[t], bkf(pb), [("ps", pb)], [("u", t)], after=marks["pb"])

        def unit(i):
            return i // NTB, i % NTB

        def s1(i, part):
            h, tb = unit(i)
            cs = slice(tb * 512, (tb + 1) * 512)
            wb = wbuf[h % 2]
            hdeps = [("hT", t) for t in range(tb * 4, tb * 4 + 4)]

            def proj(bk, c0):
                def f(e):
                    for kc in range(8):
                        ins = e.matmul(bkf(bk), wb[:, kc, c0:c0 + 128], hT[:, kc, cs], start=(kc == 0), stop=(kc == 7))
                    return ins
                A("pe", f, reads=[("wbuf", h % 2)] + hdeps, writes=[("ps", bk)])
            r2 = i % 2
            nxt = [4 * (i + 1) + j for j in range(4)] if i + 1 < NTB else []

            def hook(j):
                if nxt:
                    pb_sq(nxt[j] + 2)
                    pb_copy(nxt[j])
                    ops = rope_dve_ops(i + 1)
                    for f in ops[3 * j:3 * j + 3]:
                        f()
            if part == 0:
                if 1 <= i < NTB:
                    for hf in range(2):
                        cs2 = slice(tb * 512 + hf * 256, tb * 512 + (hf + 1) * 256)

                        def fh(e, hf=hf, cs2=cs2):
                            for kc in range(8):
                                ins = e.matmul(bkf(PB0)[:, hf * 256:(hf + 1) * 256], wb[:, kc, 0:128], hT[:, kc, cs2],
                                               start=(kc == 0), stop=(kc == 7))
                            return ins
                        A("pe", fh, reads=[("wbuf", h % 2)] + [("hT", tb * 4 + 2 * hf), ("hT", tb * 4 + 2 * hf + 1)],
                          writes=[("ps", PB0)])
                else:
                    proj(PB0, 0)
                A("act", lambda e: e.activation(qraw[r2], bkf(PB0), AF.Copy), reads=[("ps", PB0)], writes=[("qraw", r2)])
                deferred.append(lambda: A("dve", lambda e: e.tensor_tensor(aq, bkf(PB0), cosT[:, cs], ALU.mult),
                                          reads=[("ps", PB0), ("cosT", tb)], writes=["aq"]))
                hook(0)
            elif part == 1:
                proj(PB1, 128)
                A("act", lambda e: e.activation(kraw[r2], bkf(PB1), AF.Copy), reads=[("ps", PB1)], writes=[("kraw", r2)])
                A("dve", lambda e: e.tensor_tensor(ak2[r2], bkf(PB1), cosT[:, cs], ALU.mult),
                  reads=[("ps", PB1), ("cosT", tb)], writes=[("ak", r2)])
                hook(1)
            elif part == 2:
                proj(PBV, 256)
                A("act", lambda e: e.activation(vTb[r2], bkf(PBV), AF.Copy), reads=[("ps", PBV)], writes=[("vTb", r2)])
                hook(2)
            else:
                proj(PBG, 384)
                sg = sgr[i % NSG]
                A("act", lambda e: e.activation(sg, bkf(PBG), AF.Silu), reads=[("ps", PBG)], writes=[("sgr", i % NSG)])
                hook(3)
                if nxt:
                    rope_act(i + 1)
                    for t_ in nxt:
                        pb_tr(t_)
                if tb == NTB - 1 and h + 2 < H:
                    load_wblock(h % 2, (h + 2) * 512, 512, d_w[h % 2])
                if tb == NTB - 1 and h == H - 2:
                    load_wblock(0, 4 * D, 512, d_w[0])
                if tb == NTB - 1 and h == H - 1:
                    load_wblock(1, 5 * D, 512, d_w[1])
                if i == 8:
                    load_poolw()
                if i == 1:
                    load_wblock(1, 512, 512, d_w[1])
                if i == 3:
                    A("pool", lambda e: e.dma_start(out=cb[:, 384:1920], in_=constb[:, 384:1920]), writes=["cbA"],
                      dma_sem=d_c[2])

        def s2q(i):
            h, tb = unit(i)
            cs = slice(tb * 512, (tb + 1) * 512)
            r2 = i % 2
            A("pe", lambda e: e.matmul(bkf(PSW), Pm, qraw[r2], start=True, stop=True),
              reads=[("qraw", r2), "cb"], writes=[("ps", PSW)])
            A("dve", lambda e: e.tensor_tensor(bq, bkf(PSW), sinT[:, cs], ALU.mult),
              reads=[("ps", PSW), ("sinT", tb, 0), ("sinT", tb, 1)], writes=["bq"])
            A("pool", lambda e: e.tensor_tensor(qT[i % NQ], aq, bq, ALU.add),
              reads=["aq", "bq"], writes=[("qT", i % NQ)])

        def s2k(i):
            h, tb = unit(i)
            cs = slice(tb * 512, (tb + 1) * 512)
            r2 = i % 2
            A("pe", lambda e: e.matmul(bkf(PSW), Pm, kraw[r2], start=True, stop=True),
              reads=[("kraw", r2), "cb"], writes=[("ps", PSW)])
            A("dve", lambda e: e.tensor_tensor(bk, bkf(PSW), sinT[:, cs], ALU.mult),
              reads=[("ps", PSW), ("sinT", tb, 0), ("sinT", tb, 1)], writes=["bk"])
            A("pool", lambda e: e.tensor_tensor(kT[i % NK], ak2[r2], bk, ALU.add),
              reads=[("ak", r2), "bk"], writes=[("kT", i % NK)])

        def s3a(i):
            h, tb = unit(i)
            r2 = i % 2
            vt = vTb[r2]; k_ = kT[i % NK]; q_ = qT[i % NQ]

            def trv(e):
                for j in range(4):
                    ins = e.transpose(bkb(PX)[:, j * 128:(j + 1) * 128], vt[:, j * 128:(j + 1) * 128], ident)
                return ins
            A("pe", trv, reads=[("vTb", r2), "cb"], writes=[("ps", PX, 0)])
            A("act", lambda e: e.activation(vsb[i % NV].rearrange("p a b -> p (a b)"), bkb(PX)[:, 0:512], AF.Copy),
              reads=[("ps", PX, 0)], writes=[("vsb", i % NV)])

            def sc(e):
                for j in range(4):
                    ins = e.matmul(bkf(PSK)[:, j * 128:(j + 1) * 128], k_[:, j * 128:(j + 1) * 128],
                                   q_[:, j * 128:(j + 1) * 128], start=True, stop=True)
                return ins
            A("pe", sc, reads=[("kT", i % NK), ("qT", i % NQ)], writes=[("ps", PSK)])
            A("dve", lambda e: e.scalar_tensor_tensor(
                SsT[r2].rearrange("p (a b) -> p a b", a=4),
                bkf(PSK).rearrange("p (a b) -> p a b", a=4),
                kscale(h), bc(mask, 4), ALU.mult, ALU.mult),
              reads=[("ps", PSK), "cf", "cb"], writes=[("SsT", r2)])

        def s3b(i):
            h, tb = unit(i)
            r2 = i % 2
            k_ = kT[i % NK]

            def trk(e):
                for j in range(4):
                    ins = e.transpose(bkb(PX)[:, 512 + j * 128:512 + (j + 1) * 128], k_[:, j * 128:(j + 1) * 128], ident)
                return ins
            A("pe", trk, reads=[("kT", i % NK), "cb"], writes=[("ps", PX, 1)])
            A("act", lambda e: e.activation(ktl[r2].rearrange("p a b -> p (a b)"), bkb(PX)[:, 512:1024], AF.Copy,
                                            scale=kscale(h)),
              reads=[("ps", PX, 1), "cf"], writes=[("kt", r2)])

        def s4(i):
            h, tb = unit(i)
            r2 = i % 2
            kt_ = ktl[r2]; v_ = vsb[i % NV]; Rb = Rbf[i % NR]

            def kv(e):
                for j in range(4):
                    ins = e.matmul(bkf(PKV)[:, j * 128:(j + 1) * 128], kt_[:, j, :], v_[:, j, :], start=True, stop=True)
                return ins
            A("pe", kv, reads=[("kt", r2), ("vsb", i % NV)], writes=[("ps", PKV)])
            for j in range(4):
                c = tb * 4 + j
                pk = bkf(PKV)[:, j * 128:(j + 1) * 128]
                Tn = Tst[h % 2][c % 2]; Tp = Tst[h % 2][(c - 1) % 2]
                if c == 0:
                    A("dve", lambda e, pk=pk, Tn=Tn: e.tensor_copy(Tn[:, :], pk),
                      reads=[("ps", PKV)], writes=[("T", h % 2, c % 2)])
                else:
                    A("dve", lambda e, pk=pk, Tn=Tn, Tp=Tp: e.scalar_tensor_tensor(Tn[:, :], Tp[:, :], g128[h], pk,
                                                                                  ALU.mult, ALU.add),
                      reads=[("ps", PKV), ("T", h % 2, (c - 1) % 2)], writes=[("T", h % 2, c % 2)])
                A("pool", lambda e, j=j, Tn=Tn: e.tensor_scalar(Rb[:, j, :], Tn[:, :], g128[h], 0.0, ALU.mult, ALU.add),
                  reads=[("T", h % 2, c % 2)], writes=[("Rbf", i % NR, j)])

        def s5a(i):
            h, tb = unit(i)
            r2 = i % 2
            v_ = vsb[i % NV]; q_ = qT[i % NQ]; S_ = SsT[r2]
            rdeps = [("SsT", r2), ("vsb", i % NV), ("qT", i % NQ)] + [("Rbf", i % NR, j) for j in range(3)]
            if tb > 0:
                rdeps.append(("Rbf", (i - 1) % NR, 3))

            def om(e):
                for j in range(4):
                    po = bkf(PO)[:, j * 128:(j + 1) * 128]
                    first = (tb == 0 and j == 0)
                    ins = e.matmul(po, S_[:, j * 128:(j + 1) * 128], v_[:, j, :], start=True, stop=first)
                    if not first:
                        Rprev = Rbf[i % NR][:, j - 1, :] if j > 0 else Rbf[(i - 1) % NR][:, 3, :]
                        ins = e.matmul(po, q_[:, j * 128:(j + 1) * 128], Rprev, start=False, stop=True)
                return ins
            A("pe", om, reads=rdeps, writes=[("ps", PO)])
            st = bst[r2]; mv = bmv[r2]
            for j in range(4):
                A("dve", lambda e, j=j: e.bn_stats(st[:, j, :], bkf(PO)[:, j * 128:(j + 1) * 128]),
                  reads=[("ps", PO)], writes=[("bst", r2, j)])
                A("dve", lambda e, j=j: e.bn_aggr(mv[:, j, :], st[:, j, :]),
                  reads=[("bst", r2, j)], writes=[("bmv", r2, j)])
            mvd = [("bmv", r2, j) for j in range(4)]
            A("dve", lambda e: e.tensor_scalar(gve[r2][:, :], mv[:, :, 1], epsc(h), None, ALU.add),
              reads=mvd + ["cf"], writes=[("gve", r2)])
            A("pool", lambda e: e.tensor_tensor(grs[r2][:, :], gve[r2][:, :], mhalf[:, 0:4], ALU.pow),
              reads=[("gve", r2), "mhalf"], writes=[("grs", r2)])

        def s5b(i):
            h, tb = unit(i)
            r2 = i % 2
            mv = bmv[r2]
            mvd = [("bmv", r2, j) for j in range(4)]
            A("dve", lambda e: e.scalar_tensor_tensor(gnb[r2][:, :], mv[:, :, 0], -1.0, grs[r2][:, :], ALU.mult, ALU.mult),
              reads=mvd + [("grs", r2)], writes=[("gnb", r2)])
            for j in range(4):
                A("act", lambda e, j=j: e.activation(onb[r2][:, j, :], bkf(PO)[:, j * 128:(j + 1) * 128], AF.Identity,
                                                     bias=gnb[r2][:, j:j + 1], scale=grs[r2][:, j:j + 1]),
                  reads=[("ps", PO), ("gnb", r2), ("grs", r2)], writes=[("on", r2, j)])

        first_y = [True]
        deferred = []

        def s6(i):
            h, tb = unit(i)
            r2 = i % 2
            cs = slice(tb * 512, (tb + 1) * 512)
            on_ = onb[r2]

            def tro(e):
                for j in range(4):
                    ins = e.transpose(bkb(POT)[:, j * 128:(j + 1) * 128], on_[:, j, :], ident)
                return ins
            A("pe", tro, reads=[("on", r2, j) for j in range(4)] + ["cb"], writes=[("ps", POT, 0)])
            wr = [("yT", h, tb)]
            if first_y[0]:
                wr += ["ropetmp"] + [("posi", b) for b in range(4)]
                first_y[0] = False
            A("dve", lambda e: e.scalar_tensor_tensor(yT[:, h, cs], bkb(POT)[:, 0:512], normg(h), sgr[i % NSG],
                                                      ALU.mult, ALU.mult),
              reads=[("ps", POT, 0), "vf", ("sgr", i % NSG)], writes=wr)

        for i in range(NU + 5):
            if i == 0:
                rope_dve(0)
                pb_sq(0)
                pb_sq(1)
                for t in range(4):
                    pb_copy(t)
                    if t + 2 < 4:
                        pb_sq(t + 2)
                    pb_tr(t)
                pb_sq(4)
                pb_sq(5)
                rope_act(0)
            ok = lambda u: 0 <= u < NU
            if ok(i - 1): s2q(i - 1)
            if ok(i - 5): s6(i - 5)
            if ok(i - 4): s5a(i - 4)
            dr = i - NU
            def big(part):
                if i < NU:
                    s1(i, part)
                elif dr < 4:
                    u_tile(4 * dr + part)
            big(0)
            if ok(i - 2): s3a(i - 2)
            while deferred:
                deferred.pop(0)()
            big(1)
            if ok(i - 2): s3b(i - 2)
            if ok(i - 4): s5b(i - 4)
            if ok(i - 3): s4(i - 3)
            big(2)
            if ok(i - 1): s2k(i - 1)
            big(3)
            if i == NTB - 1:
                marks["pb"] = Sc.mark()

        ret_done = Sc.mark()

        for q4 in range(4):
            src = bass.AP(w_out, q4 * 512 * D, [[D, 128], [128 * D, 4], [1, D]])
            A("pool", lambda e, q4=q4, src=src: e.dma_start(out=wsb[:, q4 * 4:(q4 + 1) * 4, :], in_=src),
              writes=[("wsb", q4)], dma_sem=d_wo, after=ret_done)

        for half in range(2):
            if half == 1:
                load_wblock(0, UCOL + half * 512, 512, d_w[0])
                load_wblock(1, GCOL + half * 512, 512, d_w[1])
            if half == 1:
                for t in range(NT):
                    u_tile(t)
            for tb in range(NTB):
                for cc in range(4):
                    g = 2 * half + cc // 2
                    pbk = (tb * 4 + cc) % 4
                    def fm(e, tb=tb, cc=cc, g=g, pbk=pbk):
                        for tt in range(4):
                            t = tb * 4 + tt
                            o_ = bkf(pbk)[:, tt * 128:(tt + 1) * 128]
                            ins = e.matmul(o_, u_t[t][:, cc * 128:(cc + 1) * 128], Amat(g, 0 if t == 0 else 1),
                                           start=True, stop=(t == 0))
                            if t > 0:
                                ins = e.matmul(o_, u_t[t - 1][:, cc * 128:(cc + 1) * 128], Amat(g, 2),
                                               start=False, stop=True)
                        return ins
                    rd = [("u", t) for t in range(max(0, tb * 4 - 1), tb * 4 + 4)] + ["cbA"]
                    A("pe", fm, reads=rd, writes=[("ps", pbk)])
                    evac_copy(mixT[:, cc, tb * 512:(tb + 1) * 512], bkf(pbk), [("ps", pbk)], [("mix", cc, tb)], after=ret_done)
            for f in range(4):
                g = 2 * half + f // 2
                dd = f % 2
                for tb in range(NTB):
                    cs = slice(tb * 512, (tb + 1) * 512)
                    n_ = (f * 4 + tb)
                    pg = 4 + n_ % 2
                    pp = 6 + n_ % 2
                    def fg(e, f=f, cs=cs, pg=pg):
                        for kc in range(8):
                            ins = e.matmul(bkf(pg), wbuf[1][:, kc, f * 128:(f + 1) * 128], hT[:, kc, cs],
                                           start=(kc == 0), stop=(kc == 7))
                        return ins
                    A("pe", fg, reads=[("wbuf", 1)] + [("hT", t) for t in range(tb * 4, tb * 4 + 4)],
                      writes=[("ps", pg)])
                    sg = sgp[n_ % 3]
                    A("act", lambda e, sg=sg, pg=pg: e.activation(sg, bkf(pg), AF.Silu),
                      reads=[("ps", pg)], writes=[("sgp", n_ % 3)])
                    def fp(e, g=g, dd=dd, f=f, cs=cs, pp=pp):
                        for c2 in range(2):
                            ins = e.matmul(bkf(pp), pwv(g, c2, dd), mixT[:, 2 * (f // 2) + c2, cs],
                                           start=(c2 == 0), stop=(c2 == 1))
                        return ins
                    A("pe", fp, reads=[("pw", g // 2), ("mix", 2 * (f // 2), tb), ("mix", 2 * (f // 2) + 1, tb)],
                      writes=[("ps", pp)])
                    ych = 8 + half * 4 + f
                    A("dve", lambda e, ych=ych, cs=cs, pp=pp, sg=sg: e.scalar_tensor_tensor(
                        yT[:, ych, cs], bkf(pp), pscale(ych - 8), sg, ALU.mult, ALU.mult),
                      reads=[("ps", pp), "vf", ("sgp", n_ % 3)], writes=[("yT", ych, tb)], after=marks["pb"])

        Sc.barrier()

        A("sp", lambda e: e.dma_start(out=gpb, in_=bass.AP(gpost, 0, [[0, 128], [1, D]])), writes=["gpb"], dma_sem=d_c[4])
        for t in range(NT):
            xs_ = xst[t % 3]
            A("sp", lambda e, t=t, xs_=xs_: e.dma_start(out=xs_[:, :], in_=x[t * 128:(t + 1) * 128, :]),
              writes=[("xst", t % 3)], dma_sem=d_x[t % 3])
            b0 = (t % 2) * 2
            def fo(e, t=t, b0=b0):
                for n in range(2):
                    for fch in range(16):
                        ins = e.matmul(bkf(b0 + n), yT[:, fch, t * 128:(t + 1) * 128], wsb[:, fch, n * 512:(n + 1) * 512],
                                       start=(fch == 0), stop=(fch == 15))
                return ins
            A("pe", fo, reads=[("wsb", q4) for q4 in range(4)] + [("yT", c, t // 4) for c in range(16)],
              writes=[("ps", b0), ("ps", b0 + 1)])
            r2 = t % 2
            for n in range(2):
                A("act", lambda e, n=n, b0=b0, r2=r2: e.activation(ftmp[r2][:, n * 512:(n + 1) * 512], bkf(b0 + n), AF.Square,
                                                                   accum_out=fss[r2][:, n:n + 1]),
                  reads=[("ps", b0 + n)], writes=[("ftmp", r2, n), ("fss", r2, n)])
            A("dve", lambda e, r2=r2: e.tensor_tensor(fss[r2][:, 2:3], fss[r2][:, 0:1], fss[r2][:, 1:2], ALU.add),
              reads=[("fss", r2, 0), ("fss", r2, 1)], writes=[("fss", r2, 2)])
            A("dve", lambda e, r2=r2: e.tensor_scalar(fss[r2][:, 3:4], fss[r2][:, 2:3], 1.0 / D, EPS, ALU.mult, ALU.add),
              reads=[("fss", r2, 2)], writes=[("fss", r2, 3)])
            A("pool", lambda e, r2=r2: e.tensor_tensor(frs[r2][:, 0:1], fss[r2][:, 3:4], mhalf[:, 0:1], ALU.pow),
              reads=[("fss", r2, 3), "mhalf"], writes=[("frs", r2)])
            for n in range(2):
                hs = slice(n * 512, (n + 1) * 512)
                A("dve", lambda e, n=n, b0=b0, r2=r2, hs=hs: e.scalar_tensor_tensor(
                    ftmp[r2][:, hs], bkf(b0 + n), frs[r2][:, 0:1], gpb[:, hs],
                    ALU.mult, ALU.mult),
                  reads=[("ps", b0 + n), ("frs", r2), "gpb"], writes=[("ftmp", r2, n)])
                r3 = t % 3
                A("dve", lambda e, r2=r2, r3=r3, xs_=xs_, hs=hs: e.tensor_tensor(fout[r3][:, hs], ftmp[r2][:, hs], xs_[:, hs], ALU.add),
                  reads=[("ftmp", r2, n), ("xst", t % 3)], writes=[("fout", r3, n)])
                A("sp", lambda e, t=t, r3=r3, hs=hs: e.dma_start(out=out[t * 128:(t + 1) * 128, hs], in_=fout[r3][:, hs]),
                  reads=[("fout", r3, n)], dma_sem=d_o[r3][n])

        Sc.emit(sems, final_waits=[(d_o[a][b], 16 * len([t for t in range(NT) if t % 3 == a])) for a in range(3) for b in range(2)])
    return nc


def _consts():
    gam = 1.0 - 2.0 ** (-5.0 - np.arange(H, dtype=np.float64))
    s = np.arange(128, dtype=np.float64)
    cf = np.zeros((128, 24), np.float64)
    cf[:, 0:8] = gam[None, :] ** (-(s[:, None] + 1.0)) * (128.0 ** -0.5)
    cf[:, 8:16] = EPS * gam[None, :] ** (-2.0 * (s[:, None] + 1.0))
    half = 64
    inv_freq = 10000.0 ** (-np.arange(half, dtype=np.float64) / half)
    f_hi = inv_freq.astype(np.float32).astype(np.float64)
    f_lo = inv_freq - f_hi
    cf[:, 16] = np.concatenate([f_hi, f_hi])
    cf[:, 17] = np.concatenate([f_lo, f_lo])
    cb = np.zeros((128, 15 * 128), np.float64)
    cb[:, 0:128] = np.eye(128)
    P = np.zeros((128, 128))
    for m in range(128):
        P[(m + 64) % 128, m] = 1.0
    cb[:, 128:256] = P
    ss, cc = np.meshgrid(np.arange(128), np.arange(128), indexing="ij")
    cb[:, 256:384] = (cc >= ss).astype(np.float64)
    for g, w in enumerate(WINDOWS):
        t = cc; s_ = ss
        cnt = np.minimum(t + 1, w).astype(np.float64)
        a0 = ((s_ <= t) & (s_ > t - w)) / cnt - (s_ == t)
        a1 = ((s_ <= t) & (s_ > t - w)) / float(w) - (s_ == t)
        a2 = ((s_ - 128) > (t - w)) / float(w)
        cb[:, (3 + 3 * g + 0) * 128:(3 + 3 * g + 1) * 128] = a0
        cb[:, (3 + 3 * g + 1) * 128:(3 + 3 * g + 2) * 128] = a1
        cb[:, (3 + 3 * g + 2) * 128:(3 + 3 * g + 3) * 128] = a2
    return cf.astype(np.float32), cb.astype(np.float32)


_PROGRAM = None


def kernel(x, positions, w_in, w_out, pool_w, pool_scale, ret_norm_g, pre_norm_g, post_norm_g):
    global _PROGRAM
    x = np.asarray(x); positions = np.asarray(positions)
    w_in = np.asarray(w_in)[0]; w_out = np.asarray(w_out)[0]; pool_w = np.asarray(pool_w)[0]
    B = x.shape[0]
    assert B == 8 and x.shape[1] == S and x.shape[2] == D
    blocks = []
    for h in range(H):
        for p in (0, 1, 2, 3):
            blocks.append(w_in[:, p * D + h * 128: p * D + (h + 1) * 128])
    blocks.append(w_in[:, 4 * D:5 * D])
    blocks.append(w_in[:, 5 * D:6 * D])
    w_in_r = np.ascontiguousarray(np.concatenate(blocks, axis=1), dtype=np.float32)
    vecs = np.zeros((128, 24), np.float32)
    vecs[:, 0:8] = np.asarray(pre_norm_g)[0].reshape(8, 128).T
    vecs[:, 8:16] = np.asarray(ret_norm_g)[0].reshape(8, 128).T
    vecs[:, 16:24] = np.asarray(pool_scale)[0].reshape(8, 128).T
    gpost = np.ascontiguousarray(np.asarray(post_norm_g)[0].reshape(1, D), dtype=np.float32)
    cf, cb = _consts()
    if _PROGRAM is None:
        _PROGRAM = build_program()
    nc = _PROGRAM
    in_maps = []
    for b in range(B):
        in_maps.append({
            "x": np.ascontiguousarray(x[b], dtype=np.float32),
            "pos": np.ascontiguousarray(positions[b].reshape(1, S), dtype=np.int32),
            "w_in": w_in_r,
            "w_out": np.ascontiguousarray(w_out, dtype=np.float32),
            "pool_w": np.ascontiguousarray(pool_w.reshape(4 * 256, 256), dtype=np.float32),
            "vecs": vecs, "gpost": gpost, "constf": cf, "constb": cb,
        })
    res = run_bass_kernel_spmd(nc, in_maps, core_ids=list(range(B)))
    return np.stack([np.asarray(r["out"]).reshape(S, D) for r in res.results], axis=0).astype(np.float32)


if __name__ == "__main__":
    import time
    t0 = time.time()
    nc = build_program()
    print("built in", time.time() - t0)
```

```python
import math
from contextlib import ExitStack

import numpy as np
import concourse.bass as bass
import concourse.mybir as mybir
from concourse.bass_utils import run_bass_kernel_spmd

F32 = mybir.dt.float32
BF16 = mybir.dt.bfloat16
I32 = mybir.dt.int32
AF = mybir.ActivationFunctionType
ALU = mybir.AluOpType

D = 1024
S = 2048
H = 8
NT = 16
NTB = 4
EPS = 1e-6
WINDOWS = (2, 4, 8, 16)
TWO_PI = 2.0 * math.pi
CW1 = 6.28125
CW2 = TWO_PI - CW1
PI_SAFE = 3.1415925


class _Op:
    __slots__ = ("eng", "fn", "deps", "is_dma", "dma_sem", "dma_val", "signal", "count", "name", "idx", "waits", "know")


class Sched:
    ENGS = ("pe", "act", "dve", "pool", "sp")

    def __init__(self, nc):
        self.nc = nc
        self.ops = []
        self.last_writer = {}
        self.readers = {}
        self.dma_sem_count = {}
        self.barrier_deps = []
        self.last_of_eng = {}
        self.last_of_dsem = {}
        self.excl_last = {}

    def add(self, eng, fn, reads=(), writes=(), dma_sem=None, name="", after=()):
        op = _Op()
        op.eng = eng; op.fn = fn; op.name = name
        op.is_dma = dma_sem is not None
        op.dma_sem = dma_sem
        op.signal = False; op.count = 0; op.dma_val = 0
        op.idx = len(self.ops)
        deps = set(self.barrier_deps)
        deps.update(after)
        for r in reads:
            w = self.last_writer.get(r)
            if w is not None:
                deps.add(w)
        for w_ in writes:
            lw = self.last_writer.get(w_)
            if lw is not None:
                deps.add(lw)
            for rd in self.readers.get(w_, ()):
                deps.add(rd)
        banks = set()
        for r in list(reads) + list(writes):
            if isinstance(r, tuple) and r[0] == "ps":
                banks.add(r[1])
        for b in banks:
            g = self.excl_last.get(b)
            if g is not None and g[0] != eng:
                deps.add(g[1])
            self.excl_last[b] = (eng, op)
        deps.discard(op)
        op.deps = deps
        for r in reads:
            self.readers.setdefault(r, []).append(op)
        for w_ in writes:
            self.last_writer[w_] = op
            self.readers[w_] = []
        if op.is_dma:
            c = self.dma_sem_count.get(id(dma_sem), 0) + 16
            self.dma_sem_count[id(dma_sem)] = c
            op.dma_val = c
            self.last_of_dsem[id(dma_sem)] = op
        else:
            self.last_of_eng[eng] = op
        self.ops.append(op)
        return op

    def mark(self):
        return list(self.last_of_eng.values()) + list(self.last_of_dsem.values())

    def barrier(self):
        self.barrier_deps = list(self.last_of_eng.values()) + list(self.last_of_dsem.values())

    def emit(self, sems, final_waits=()):
        nc = self.nc
        for op in self.ops:
            for d in op.deps:
                if d.is_dma:
                    continue
                if d.eng == "pe" and op.eng == "pe" and not op.is_dma:
                    continue
                d.signal = True
        counts = {e: 0 for e in self.ENGS}
        for op in self.ops:
            if op.is_dma:
                continue
            if op.signal:
                counts[op.eng] += 1
                op.count = counts[op.eng]
        per_eng = {e: [o for o in self.ops if o.eng == e] for e in self.ENGS}

        eng_know = {e: {} for e in self.ENGS}
        self.n_waits = 0
        self.n_skipped = 0
        for op in self.ops:
            need = {}
            for d in op.deps:
                if d.is_dma:
                    key = ("dma", id(d.dma_sem)); sem = d.dma_sem; val = d.dma_val
                else:
                    if d.eng == "pe" and op.eng == "pe" and not op.is_dma:
                        continue
                    key = d.eng; sem = sems[d.eng]; val = d.count
                if val > need.get(key, (None, 0, None))[1]:
                    need[key] = (sem, val, d)
            know = eng_know[op.eng]
            waits = []
            for key, (sem, val, d) in sorted(need.items(), key=lambda kv: -kv[1][1]):
                if know.get(key, 0) >= val:
                    self.n_skipped += 1
                    continue
                waits.append((sem, val))
                self.n_waits += 1
                know[key] = val
                for k2, v2 in d.know.items():
                    if v2 > know.get(k2, 0):
                        know[k2] = v2
            op.waits = waits
            op.know = dict(know)
            if op.is_dma:
                op.know[("dma", id(op.dma_sem))] = op.dma_val
            elif op.signal:
                op.know[op.eng] = max(op.know.get(op.eng, 0), op.count)

        def run_stream(e, engobj):
            for op in per_eng[e]:
                for sem, val in op.waits:
                    engobj.wait_ge(sem, val)
                ins = op.fn(engobj)
                if op.is_dma:
                    ins.then_inc(op.dma_sem, 16)
                elif op.signal:
                    ins.then_inc(sems[op.eng], 1)
            if e == "sp":
                for sem, val in final_waits:
                    engobj.wait_ge(sem, val)

        with nc.Block() as block:
            @block.tensor
            def _(eng):
                run_stream("pe", eng)

            @block.scalar
            def _(eng):
                run_stream("act", eng)

            @block.vector
            def _(eng):
                run_stream("dve", eng)

            @block.gpsimd
            def _(eng):
                run_stream("pool", eng)

            @block.sync
            def _(eng):
                run_stream("sp", eng)


def bc(ap2d, n_mid):
    (ps, pn), (st, n) = ap2d.ap
    return bass.AP(ap2d.tensor, ap2d.offset, [[ps, pn], [0, n_mid], [st, n]])


def col_bc(ap2d, n_in):
    (ps, pn), (st, k) = ap2d.ap
    return bass.AP(ap2d.tensor, ap2d.offset, [[ps, pn], [st, k], [0, n_in]])


def build_program():
    nc = bass.Bass("TRN2", target_bir_lowering=False)
    x = nc.dram_tensor("x", [S, D], F32, kind="ExternalInput")
    pos = nc.dram_tensor("pos", [1, S], I32, kind="ExternalInput")
    w_in = nc.dram_tensor("w_in", [D, 6 * D], F32, kind="ExternalInput")
    w_out = nc.dram_tensor("w_out", [2 * D, D], F32, kind="ExternalInput")
    pool_w = nc.dram_tensor("pool_w", [4 * 256, 256], F32, kind="ExternalInput")
    vecs = nc.dram_tensor("vecs", [128, 24], F32, kind="ExternalInput")
    gpost = nc.dram_tensor("gpost", [1, D], F32, kind="ExternalInput")
    constf = nc.dram_tensor("constf", [128, 24], F32, kind="ExternalInput")
    constb = nc.dram_tensor("constb", [128, 15 * 128], F32, kind="ExternalInput")
    out = nc.dram_tensor("out", [S, D], F32, kind="ExternalOutput")

    gam = [1.0 - 2.0 ** (-5.0 - h) for h in range(H)]
    g128 = [g ** 128 for g in gam]

    with ExitStack() as ctx:
        def sb(name, shape, dt):
            return ctx.enter_context(nc.sbuf_tensor(name, shape, dt))

        def sem(name):
            return ctx.enter_context(nc.semaphore(name))

        hT = sb("hT", [128, 8, S], BF16)
        yT = sb("yT", [128, 16, S], BF16)
        RW = sb("RW", [128, 16384], BF16)
        RP = sb("RP", [128, 18432], BF16)
        wbuf = [sb(f"wbuf{i}", [128, 8, 512], BF16) for i in range(2)]
        xst = [sb(f"xst{i}", [128, D], F32) for i in range(3)]
        xn01 = [sb(f"xn{i}", [128, D], BF16) for i in range(2)]
        cb = sb("cb", [128, 15 * 128], BF16)
        cf = sb("cf", [128, 24], F32)
        vf = sb("vf", [128, 24], F32)
        Tst = [[sb(f"Tst{i}_{k}", [128, 128], F32) for k in range(2)] for i in range(2)]
        ssq = sb("ssq", [128, 16], F32)
        rstd = sb("rstd", [128, 16], F32)
        ve_ = sb("ve", [128, 16], F32)
        mhalf = sb("mhalf", [128, 16], F32)
        halfpi = sb("halfpi", [128, 1], F32)
        actwarm = sb("actwarm", [128, 1], F32)
        bst = [sb(f"bst{i}", [128, 4, 6], F32) for i in range(2)]
        bmv = [sb(f"bmv{i}", [128, 4, 2], F32) for i in range(2)]
        gve = [sb(f"gve{i}", [128, 4], F32) for i in range(2)]
        grs = [sb(f"grs{i}", [128, 4], F32) for i in range(2)]
        gnb = [sb(f"gnb{i}", [128, 4], F32) for i in range(2)]
        fss = [sb(f"fss{i}", [128, 4], F32) for i in range(2)]
        frs = [sb(f"frs{i}", [128, 2], F32) for i in range(2)]

        cosT = RW[:, 0:4096].bitcast(F32)
        sinT = RW[:, 4096:8192].bitcast(F32)
        aq = RW[:, 8192:9216].bitcast(F32)
        bq = RW[:, 9216:10240].bitcast(F32)
        ak2 = [RW[:, 10240:11264].bitcast(F32), RW[:, 11264:12288].bitcast(F32)]
        bk = RW[:, 12288:13312].bitcast(F32)
        NQ, NK = 4, 2
        qT = [RW[:, 13312 + 512 * i:13312 + 512 * (i + 1)] for i in range(NQ)]
        kT = [RW[:, 15360 + 512 * i:15360 + 512 * (i + 1)] for i in range(NK)]
        wsb = RW[:, :].rearrange("p (a b) -> p a b", a=16)

        def rp(off, n):
            return RP[:, off:off + n]
        o = 0
        qraw = [rp(o + 512 * i, 512) for i in range(2)]; o += 1024
        kraw = [rp(o + 512 * i, 512) for i in range(2)]; o += 1024
        vTb = [rp(o + 512 * i, 512) for i in range(2)]; o += 1024
        NSG = 7
        sgr = [rp(o + 512 * i, 512) for i in range(NSG)]; o += 512 * NSG
        NV = 4
        vsb = [rp(o + 512 * i, 512).rearrange("p (a b) -> p a b", a=4) for i in range(NV)]; o += 512 * NV
        ktl = [rp(o + 512 * i, 512).rearrange("p (a b) -> p a b", a=4) for i in range(2)]; o += 1024
        SsT = [rp(o + 512 * i, 512) for i in range(2)]; o += 1024
        NR = 3
        Rbf = [rp(o + 512 * i, 512).rearrange("p (a b) -> p a b", a=4) for i in range(NR)]; o += 512 * NR
        onb = [rp(o + 512 * i, 512).rearrange("p (a b) -> p a b", a=4) for i in range(2)]; o += 1024
        assert o <= 13312, o
        sqj = RP[:, 13312:14336]
        xn = [xn01[0][:, :], xn01[1][:, :], RP[:, 14336:15360], RP[:, 15360:16384]]
        u_t = []
        for k in range(3):
            xb_ = xst[k][:, :].bitcast(BF16)
            u_t += [xb_[:, 512 * j:512 * (j + 1)] for j in range(4)]
        u_t += [RP[:, 13312 + 512 * j:13312 + 512 * (j + 1)] for j in range(4)]
        mixT = RP[:, 0:8192].rearrange("p (a b) -> p a b", a=4)
        sgp = [RP[:, 15360 + 512 * i:15360 + 512 * (i + 1)] for i in range(3)]
        ftmp = [RP[:, 2048 * i:2048 * (i + 1)].bitcast(F32) for i in range(2)]
        fout = [RP[:, 4096 + 2048 * i:4096 + 2048 * (i + 1)].bitcast(F32) for i in range(3)]
        gpb = RP[:, 10240:12288].bitcast(F32)
        NXB = 8
        xsB = [yT[:, 8 + k, :].bitcast(F32) for k in range(NXB)]
        yflat = yT[:, :, :].rearrange("p a b -> p (a b)")
        posi = yflat[:, 0:4096].bitcast(I32)
        rt = [yflat[:, 4096 + 1024 * i:4096 + 1024 * (i + 1)].bitcast(F32) for i in range(5)]
        rki = yflat[:, 4096 + 5120:4096 + 6144].bitcast(I32)

        ident = cb[:, 0:128]
        Pm = cb[:, 128:256]
        mask = cb[:, 256:384]
        def Amat(g, k):
            c0 = (3 + 3 * g + k) * 128
            return cb[:, c0:c0 + 128]
        kscale = lambda h: cf[:, h:h + 1]
        epsc = lambda h: cf[:, 8 + h:9 + h]
        invf = cf[:, 16:17]
        invf_lo = cf[:, 17:18]
        gpreT = vf[:, 0:8]
        normg = lambda h: vf[:, 8 + h:9 + h]
        pscale = lambda c: vf[:, 16 + c:17 + c]

        bank = [ctx.enter_context(nc.psum_tensor(f"bank{i}", [128, 512], F32)) for i in range(8)]
        def bkf(i):
            return bank[i][:, :]
        def bkb(i):
            return bank[i][:, :].bitcast(BF16)

        sems = {e: sem("s_" + e) for e in ("pe", "act", "dve", "pool")}
        d_c = [sem(f"d_c{i}") for i in range(5)]
        d_p = [sem(f"d_p{i}") for i in range(4)]
        d_x = [sem(f"d_x{i}") for i in range(3)]
        d_xb = [sem(f"d_xb{i}") for i in range(8)]
        d_w = [sem(f"d_w{i}") for i in range(2)]
        d_wo = sem("d_wo")
        d_pw = [sem(f"d_pw{i}") for i in range(2)]
        d_o = [[sem(f"d_o{i}_{n}") for n in range(2)] for i in range(3)]

        Sc = Sched(nc)
        PBQ, PBK, PBV, PBG, PSW, PX, PSK, PO = range(8)
        PB0, PB1, PTR, PSC, PKV, POT = PBQ, PBK, PX, PSK, PSK, PX
        A = Sc.add

        xload_ops = {}

        def x_load(t):
            if t >= NT or t in xload_ops:
                return
            xs_ = xsB[t % NXB]
            aft = [xload_ops[t - 2]] if (4 <= t < NXB) else []
            if t == 4 and "w0" in xload_ops:
                aft = aft + [xload_ops["w0"]]
            xload_ops[t] = A("sp", lambda e: e.dma_start(out=xs_, in_=x[t * 128:(t + 1) * 128, :]),
                             writes=[("xsB", t % NXB)], dma_sem=d_xb[t % NXB], after=aft)

        def posi_load(b):
            A("sp", lambda e: e.dma_start(out=posi[:, b * 512:(b + 1) * 512], in_=bass.AP(pos, b * 512, [[0, 128], [1, 512]])),
              writes=[("posi", b)] + (["ropetmp"] if b == 0 else []), dma_sem=d_p[b])
        posi_load(0)
        A("sp", lambda e: e.dma_start(out=cf[:, :], in_=constf[:, :]), writes=["cf"], dma_sem=d_c[0])
        A("sp", lambda e: e.dma_start(out=vf[:, :], in_=vecs[:, :]), writes=["vf"], dma_sem=d_c[1])
        for t in range(4):
            x_load(t)
        A("pool", lambda e: e.dma_start(out=cb[:, 0:384], in_=constb[:, 0:384]), writes=["cb"], dma_sem=d_c[3])
        WB0 = bass.AP(w_in, 0, [[6 * D, 128], [128 * 6 * D, 8], [1, 512]])
        xload_ops["w0"] = A("pool", lambda e: e.dma_start(out=wbuf[0][:, :, 0:512], in_=WB0), writes=[("wbuf", 0)],
                            dma_sem=d_w[0])
        for t in range(4, NXB):
            x_load(t)
        for b in range(1, 4):
            posi_load(b)
        A("dve", lambda e: e.memset(mhalf[:, :], -0.5), writes=["mhalf"])
        A("dve", lambda e: e.memset(halfpi[:, :], math.pi / 2), writes=["halfpi"])
        A("act", lambda e: e.activation(actwarm[:, :], mhalf[:, 0:1], AF.Silu), reads=["mhalf"], writes=["actwarm"])

        def load_wblock(slot, col0, ncols, dsem, after=()):
            src = bass.AP(w_in, col0, [[6 * D, 128], [128 * 6 * D, 8], [1, ncols]])
            A("pool", lambda e: e.dma_start(out=wbuf[slot][:, :, 0:ncols], in_=src),
              writes=[("wbuf", slot)], dma_sem=dsem, after=after)

        def pb_sq(t):
            if t >= NT:
                return
            xs_ = xsB[t % NXB]
            x_load(t)
            A("act", lambda e: e.activation(sqj[:, :], xs_, AF.Square, accum_out=ssq[:, t:t + 1]),
              reads=[("xsB", t % NXB)], writes=["sqj", ("ssq", t)])
            A("pool", lambda e: e.tensor_scalar(ve_[:, t:t + 1], ssq[:, t:t + 1], 1.0 / D, EPS, ALU.mult, ALU.add),
              reads=[("ssq", t)], writes=[("ve", t)])
            A("pool", lambda e: e.tensor_tensor(rstd[:, t:t + 1], ve_[:, t:t + 1], mhalf[:, 0:1], ALU.pow),
              reads=[("ve", t), "mhalf"], writes=[("rstd", t)])

        def pb_copy(t):
            xs_ = xsB[t % NXB]; xn_ = xn[t % 4]
            if t < 4:
                if t % 2 == 0:
                    A("act", lambda e: e.activation(xn_[:, :], xs_, AF.Copy, scale=rstd[:, t:t + 1]),
                      reads=[("xsB", t % NXB), ("rstd", t)], writes=[("xn", t % 4)])
                else:
                    A("pool", lambda e: e.tensor_scalar(xn_[:, :], xs_, rstd[:, t:t + 1], 0.0, ALU.mult, ALU.add),
                      reads=[("xsB", t % NXB), ("rstd", t)], writes=[("xn", t % 4)])
            else:
                A("act", lambda e: e.activation(xn_[:, 0:512], xs_[:, 0:512], AF.Copy, scale=rstd[:, t:t + 1]),
                  reads=[("xsB", t % NXB), ("rstd", t)], writes=[("xn", t % 4, 0)])
                A("pool", lambda e: e.tensor_scalar(xn_[:, 512:1024], xs_[:, 512:1024], rstd[:, t:t + 1], 0.0, ALU.mult, ALU.add),
                  reads=[("xsB", t % NXB), ("rstd", t)], writes=[("xn", t % 4, 1)])
            x_load(t + NXB)

        def pb_tr(t):
            xn_ = xn[t % 4]; pb = (PX, PSK, PO)[t % 3]

            def tr(e):
                for fc in range(8):
                    ins = e.transpose(bkb(pb)[:, fc * 128:(fc + 1) * 128], xn_[:, fc * 128:(fc + 1) * 128], ident)
                return ins
            A("pe", tr, reads=[("xn", t % 4), ("xn", t % 4, 0), ("xn", t % 4, 1), "cb"],
              writes=[("ps", pb, 0), ("ps", pb, 1), ("ps", pb)])
            A("dve", lambda e: e.tensor_tensor(
                hT[:, :, t * 128:(t + 1) * 128],
                bkb(pb).rearrange("p (a b) -> p a b", a=8),
                col_bc(gpreT, 128), ALU.mult),
              reads=[("ps", pb, 0), ("ps", pb, 1), ("ps", pb), "vf"], writes=[("hT", t)])

        def rope_dve_ops(b):
            cs = slice(b * 512, (b + 1) * 512)
            t0, t1, t2, t3, t4 = [r_[:, :] for r_ in rt]
            rd = ["ropetmp"]; wr = ["ropetmp"]
            return [
                lambda: A("dve", lambda e: e.tensor_copy(t0, posi[:, cs]), reads=[("posi", b)] + rd, writes=wr),
                lambda: A("dve", lambda e: e.tensor_scalar(t2, t0, invf, None, ALU.mult), reads=["cf"] + rd, writes=wr),
                lambda: A("dve", lambda e: e.scalar_tensor_tensor(t1, t0, invf_lo, t2, ALU.mult, ALU.add), reads=["cf"] + rd, writes=wr),
                lambda: A("dve", lambda e: e.tensor_scalar(rki[:, :], t1, 1.0 / TWO_PI, None, ALU.mult), reads=rd, writes=wr),
                lambda: A("dve", lambda e: e.tensor_copy(t0, rki[:, :]), reads=rd, writes=wr),
                lambda: A("dve", lambda e: e.scalar_tensor_tensor(t2, t0, -CW1, t1, ALU.mult, ALU.add), reads=rd, writes=wr),
                lambda: A("dve", lambda e: e.scalar_tensor_tensor(t1, t0, -CW2, t2, ALU.mult, ALU.add), reads=rd, writes=wr),
                lambda: A("dve", lambda e: e.tensor_scalar(t3, t1, PI_SAFE, -PI_SAFE, ALU.min, ALU.max), reads=rd, writes=["ropetmp", "rt3"]),
                lambda: A("dve", lambda e: e.scalar_tensor_tensor(t4, t1, -1.0, t1, ALU.mult, ALU.max), reads=rd, writes=["ropetmp", "rt4"]),
            ]

        def rope_dve(b):
            for f in rope_dve_ops(b):
                f()

        def rope_act(b):
            cs = slice(b * 512, (b + 1) * 512)
            t0, t1, t2, t3, t4 = [r_[:, :] for r_ in rt]
            A("act", lambda e: e.activation(sinT[0:64, cs], t3[0:64, :], AF.Sin, scale=-1.0),
              reads=["ropetmp", "rt3"], writes=[("sinT", b, 0)])
            A("act", lambda e: e.activation(sinT[64:128, cs], t3[64:128, :], AF.Sin),
              reads=["ropetmp", "rt3"], writes=[("sinT", b, 1)])
            A("act", lambda e: e.activation(cosT[:, cs], t4, AF.Sin, bias=halfpi[:, 0:1], scale=-1.0),
              reads=["ropetmp", "rt4", "halfpi"], writes=[("cosT", b)])

        def rope_block(b):
            rope_dve(b)
            rope_act(b)

        NU = H * NTB

        pwt = xn01[0]
        pwt2 = xn01[1]
        def pwv(g, c2, dd):
            t_ = pwt if g < 2 else pwt2
            base = ((g % 2) * 2 + c2) * 256 + dd * 128
            return t_[:, base:base + 128]

        def load_poolw():
            for gg in range(2):
                t_ = pwt if gg == 0 else pwt2
                src = bass.AP(pool_w, gg * 512 * 256, [[256, 128], [128 * 256, 4], [1, 256]])
                A("pool", lambda e, t_=t_, src=src: e.dma_start(out=t_[:, :].rearrange("p (a b) -> p a b", a=4), in_=src),
                  writes=[("pw", gg), ("xn", gg)], dma_sem=d_pw[gg])

        UCOL = 4 * D
        GCOL = 5 * D
        evq = [0]
        marks = {}

        def evac_copy(dst, src, reads, writes, after=()):
            evq[0] += 1
            if evq[0] % 2 == 0:
                A("act", lambda e: e.activation(dst, src, AF.Copy), reads=reads, writes=writes, after=after)
            else:
                A("dve", lambda e: e.tensor_copy(dst, src), reads=reads, writes=writes, after=after)

        def u_tile(t):
            pb = t % 4

            def fu(e):
                for kc in range(8):
                    ins = e.matmul(bkf(pb), hT[:, kc, t * 128:(t + 1) * 128], wbuf[0][:, kc, :],
                                   start=(kc == 0), stop=(kc == 7))
                return ins
            A("pe", fu, reads=[("wbuf", 0), ("hT", t)], writes=[("ps", pb)])
            evac_copy(u_t[t], bkf(pb), [("ps", pb)], [("u", t)], after=marks["pb"])

        def unit(i):
            return i // NTB, i % NTB

        def s1(i, part):
            h, tb = unit(i)
            cs = slice(tb * 512, (tb + 1) * 512)
            wb = wbuf[h % 2]
            hdeps = [("hT", t) for t in range(tb * 4, tb * 4 + 4)]

            def proj(bk, c0):
                def f(e):
                    for kc in range(8):
                        ins = e.matmul(bkf(bk), wb[:, kc, c0:c0 + 128], hT[:, kc, cs], start=(kc == 0), stop=(kc == 7))
                    return ins
                A("pe", f, reads=[("wbuf", h % 2)] + hdeps, writes=[("ps", bk)])
            r2 = i % 2
            nxt = [4 * (i + 1) + j for j in range(4)] if i + 1 < NTB else []

            def hook(j):
                if nxt:
                    pb_sq(nxt[j] + 2)
                    pb_copy(nxt[j])
                    ops = rope_dve_ops(i + 1)
                    for f in ops[3 * j:3 * j + 3]:
                        f()
            if part == 0:
                if 1 <= i < NTB:
                    for hf in range(2):
                        cs2 = slice(tb * 512 + hf * 256, tb * 512 + (hf + 1) * 256)

                        def fh(e, hf=hf, cs2=cs2):
                            for kc in range(8):
                                ins = e.matmul(bkf(PB0)[:, hf * 256:(hf + 1) * 256], wb[:, kc, 0:128], hT[:, kc, cs2],
                                               start=(kc == 0), stop=(kc == 7))
                            return ins
                        A("pe", fh, reads=[("wbuf", h % 2)] + [("hT", tb * 4 + 2 * hf), ("hT", tb * 4 + 2 * hf + 1)],
                          writes=[("ps", PB0)])
                else:
                    proj(PB0, 0)
                A("act", lambda e: e.activation(qraw[r2], bkf(PB0), AF.Copy), reads=[("ps", PB0)], writes=[("qraw", r2)])
                deferred.append(lambda: A("dve", lambda e: e.tensor_tensor(aq, bkf(PB0), cosT[:, cs], ALU.mult),
                                          reads=[("ps", PB0), ("cosT", tb)], writes=["aq"]))
                hook(0)
            elif part == 1:
                proj(PB1, 128)
                A("act", lambda e: e.activation(kraw[r2], bkf(PB1), AF.Copy), reads=[("ps", PB1)], writes=[("kraw", r2)])
                A("dve", lambda e: e.tensor_tensor(ak2[r2], bkf(PB1), cosT[:, cs], ALU.mult),
                  reads=[("ps", PB1), ("cosT", tb)], writes=[("ak", r2)])
                hook(1)
            elif part == 2:
                proj(PBV, 256)
                A("act", lambda e: e.activation(vTb[r2], bkf(PBV), AF.Copy), reads=[("ps", PBV)], writes=[("vTb", r2)])
                hook(2)
            else:
                proj(PBG, 384)
                sg = sgr[i % NSG]
                A("act", lambda e: e.activation(sg, bkf(PBG), AF.Silu), reads=[("ps", PBG)], writes=[("sgr", i % NSG)])
                hook(3)
                if nxt:
                    rope_act(i + 1)
                    for t_ in nxt:
                        pb_tr(t_)
                if tb == NTB - 1 and h + 2 < H:
                    load_wblock(h % 2, (h + 2) * 512, 512, d_w[h % 2])
                if tb == NTB - 1 and h == H - 2:
                    load_wblock(0, 4 * D, 512, d_w[0])
                if tb == NTB - 1 and h == H - 1:
                    load_wblock(1, 5 * D, 512, d_w[1])
                if i == 8:
                    load_poolw()
                if i == 1:
                    load_wblock(1, 512, 512, d_w[1])
                if i == 3:
                    A("pool", lambda e: e.dma_start(out=cb[:, 384:1920], in_=constb[:, 384:1920]), writes=["cbA"],
                      dma_sem=d_c[2])

        def s2q(i):
            h, tb = unit(i)
            cs = slice(tb * 512, (tb + 1) * 512)
            r2 = i % 2
            A("pe", lambda e: e.matmul(bkf(PSW), Pm, qraw[r2], start=True, stop=True),
              reads=[("qraw", r2), "cb"], writes=[("ps", PSW)])
            A("dve", lambda e: e.tensor_tensor(bq, bkf(PSW), sinT[:, cs], ALU.mult),
              reads=[("ps", PSW), ("sinT", tb, 0), ("sinT", tb, 1)], writes=["bq"])
            A("pool", lambda e: e.tensor_tensor(qT[i % NQ], aq, bq, ALU.add),
              reads=["aq", "bq"], writes=[("qT", i % NQ)])

        def s2k(i):
            h, tb = unit(i)
            cs = slice(tb * 512, (tb + 1) * 512)
            r2 = i % 2
            A("pe", lambda e: e.matmul(bkf(PSW), Pm, kraw[r2], start=True, stop=True),
              reads=[("kraw", r2), "cb"], writes=[("ps", PSW)])
            A("dve", lambda e: e.tensor_tensor(bk, bkf(PSW), sinT[:, cs], ALU.mult),
              reads=[("ps", PSW), ("sinT", tb, 0), ("sinT", tb, 1)], writes=["bk"])
            A("pool", lambda e: e.tensor_tensor(kT[i % NK], ak2[r2], bk, ALU.add),
              reads=[("ak", r2), "bk"], writes=[("kT", i % NK)])

        def s3a(i):
            h, tb = unit(i)
            r2 = i % 2
            vt = vTb[r2]; k_ = kT[i % NK]; q_ = qT[i % NQ]

            def trv(e):
                for j in range(4):
                    ins = e.transpose(bkb(PX)[:, j * 128:(j + 1) * 128], vt[:, j * 128:(j + 1) * 128], ident)
                return ins
            A("pe", trv, reads=[("vTb", r2), "cb"], writes=[("ps", PX, 0)])
            A("act", lambda e: e.activation(vsb[i % NV].rearrange("p a b -> p (a b)"), bkb(PX)[:, 0:512], AF.Copy),
              reads=[("ps", PX, 0)], writes=[("vsb", i % NV)])

            def sc(e):
                for j in range(4):
                    ins = e.matmul(bkf(PSK)[:, j * 128:(j + 1) * 128], k_[:, j * 128:(j + 1) * 128],
                                   q_[:, j * 128:(j + 1) * 128], start=True, stop=True)
                return ins
            A("pe", sc, reads=[("kT", i % NK), ("qT", i % NQ)], writes=[("ps", PSK)])
            A("dve", lambda e: e.scalar_tensor_tensor(
                SsT[r2].rearrange("p (a b) -> p a b", a=4),
                bkf(PSK).rearrange("p (a b) -> p a b", a=4),
                kscale(h), bc(mask, 4), ALU.mult, ALU.mult),
              reads=[("ps", PSK), "cf", "cb"], writes=[("SsT", r2)])

        def s3b(i):
            h, tb = unit(i)
            r2 = i % 2
            k_ = kT[i % NK]

            def trk(e):
                for j in range(4):
                    ins = e.transpose(bkb(PX)[:, 512 + j * 128:512 + (j + 1) * 128], k_[:, j * 128:(j + 1) * 128], ident)
                return ins
            A("pe", trk, reads=[("kT", i % NK), "cb"], writes=[("ps", PX, 1)])
            A("act", lambda e: e.activation(ktl[r2].rearrange("p a b -> p (a b)"), bkb(PX)[:, 512:1024], AF.Copy,
                                            scale=kscale(h)),
              reads=[("ps", PX, 1), "cf"], writes=[("kt", r2)])

        def s4(i):
            h, tb = unit(i)
            r2 = i % 2
            kt_ = ktl[r2]; v_ = vsb[i % NV]; Rb = Rbf[i % NR]

            def kv(e):
                for j in range(4):
                    ins = e.matmul(bkf(PKV)[:, j * 128:(j + 1) * 128], kt_[:, j, :], v_[:, j, :], start=True, stop=True)
                return ins
            A("pe", kv, reads=[("kt", r2), ("vsb", i % NV)], writes=[("ps", PKV)])
            for j in range(4):
                c = tb * 4 + j
                pk = bkf(PKV)[:, j * 128:(j + 1) * 128]
                Tn = Tst[h % 2][c % 2]; Tp = Tst[h % 2][(c - 1) % 2]
                if c == 0:
                    A("dve", lambda e, pk=pk, Tn=Tn: e.tensor_copy(Tn[:, :], pk),
                      reads=[("ps", PKV)], writes=[("T", h % 2, c % 2)])
                else:
                    A("dve", lambda e, pk=pk, Tn=Tn, Tp=Tp: e.scalar_tensor_tensor(Tn[:, :], Tp[:, :], g128[h], pk,
                                                                                  ALU.mult, ALU.add),
                      reads=[("ps", PKV), ("T", h % 2, (c - 1) % 2)], writes=[("T", h % 2, c % 2)])
                A("pool", lambda e, j=j, Tn=Tn: e.tensor_scalar(Rb[:, j, :], Tn[:, :], g128[h], 0.0, ALU.mult, ALU.add),
                  reads=[("T", h % 2, c % 2)], writes=[("Rbf", i % NR, j)])

        def s5a(i):
            h, tb = unit(i)
            r2 = i % 2
            v_ = vsb[i % NV]; q_ = qT[i % NQ]; S_ = SsT[r2]
            rdeps = [("SsT", r2), ("vsb", i % NV), ("qT", i % NQ)] + [("Rbf", i % NR, j) for j in range(3)]
            if tb > 0:
                rdeps.append(("Rbf", (i - 1) % NR, 3))

            def om(e):
                for j in range(4):
                    po = bkf(PO)[:, j * 128:(j + 1) * 128]
                    first = (tb == 0 and j == 0)
                    ins = e.matmul(po, S_[:, j * 128:(j + 1) * 128], v_[:, j, :], start=True, stop=first)
                    if not first:
                        Rprev = Rbf[i % NR][:, j - 1, :] if j > 0 else Rbf[(i - 1) % NR][:, 3, :]
                        ins = e.matmul(po, q_[:, j * 128:(j + 1) * 128], Rprev, start=False, stop=True)
                return ins
            A("pe", om, reads=rdeps, writes=[("ps", PO)])
            st = bst[r2]; mv = bmv[r2]
            for j in range(4):
                A("dve", lambda e, j=j: e.bn_stats(st[:, j, :], bkf(PO)[:, j * 128:(j + 1) * 128]),
                  reads=[("ps", PO)], writes=[("bst", r2, j)])
                A("dve", lambda e, j=j: e.bn_aggr(mv[:, j, :], st[:, j, :]),
                  reads=[("bst", r2, j)], writes=[("bmv", r2, j)])
            mvd = [("bmv", r2, j) for j in range(4)]
            A("dve", lambda e: e.tensor_scalar(gve[r2][:, :], mv[:, :, 1], epsc(h), None, ALU.add),
              reads=mvd + ["cf"], writes=[("gve", r2)])
            A("pool", lambda e: e.tensor_tensor(grs[r2][:, :], gve[r2][:, :], mhalf[:, 0:4], ALU.pow),
              reads=[("gve", r2), "mhalf"], writes=[("grs", r2)])

        def s5b(i):
            h, tb = unit(i)
            r2 = i % 2
            mv = bmv[r2]
            mvd = [("bmv", r2, j) for j in range(4)]
            A("dve", lambda e: e.scalar_tensor_tensor(gnb[r2][:, :], mv[:, :, 0], -1.0, grs[r2][:, :], ALU.mult, ALU.mult),
              reads=mvd + [("grs", r2)], writes=[("gnb", r2)])
            for j in range(4):
                A("act", lambda e, j=j: e.activation(onb[r2][:, j, :], bkf(PO)[:, j * 128:(j + 1) * 128], AF.Identity,
                                                     bias=gnb[r2][:, j:j + 1], scale=grs[r2][:, j:j + 1]),
                  reads=[("ps", PO), ("gnb", r2), ("grs", r2)], writes=[("on", r2, j)])

        first_y = [True]
        deferred = []

        def s6(i):
            h, tb = unit(i)
            r2 = i % 2
            cs = slice(tb * 512, (tb + 1) * 512)
            on_ = onb[r2]

            def tro(e):
                for j in range(4):
                    ins = e.transpose(bkb(POT)[:, j * 128:(j + 1) * 128], on_[:, j, :], ident)
                return ins
            A("pe", tro, reads=[("on", r2, j) for j in range(4)] + ["cb"], writes=[("ps", POT, 0)])
            wr = [("yT", h, tb)]
            if first_y[0]:
                wr += ["ropetmp"] + [("posi", b) for b in range(4)]
                first_y[0] = False
            A("dve", lambda e: e.scalar_tensor_tensor(yT[:, h, cs], bkb(POT)[:, 0:512], normg(h), sgr[i % NSG],
                                                      ALU.mult, ALU.mult),
              reads=[("ps", POT, 0), "vf", ("sgr", i % NSG)], writes=wr)

        for i in range(NU + 5):
            if i == 0:
                rope_dve(0)
                pb_sq(0)
                pb_sq(1)
                for t in range(4):
                    pb_copy(t)
                    if t + 2 < 4:
                        pb_sq(t + 2)
                    pb_tr(t)
                pb_sq(4)
                pb_sq(5)
                rope_act(0)
            ok = lambda u: 0 <= u < NU
            if ok(i - 1): s2q(i - 1)
            if ok(i - 5): s6(i - 5)
            if ok(i - 4): s5a(i - 4)
            dr = i - NU
            def big(part):
                if i < NU:
                    s1(i, part)
                elif dr < 4:
                    u_tile(4 * dr + part)
            big(0)
            if ok(i - 2): s3a(i - 2)
            while deferred:
                deferred.pop(0)()
            big(1)
            if ok(i - 2): s3b(i - 2)
            if ok(i - 4): s5b(i - 4)
            if ok(i - 3): s4(i - 3)
            big(2)
            if ok(i - 1): s2k(i - 1)
            big(3)
            if i == NTB - 1:
                marks["pb"] = Sc.mark()

        ret_done = Sc.mark()

        for q4 in range(4):
            src = bass.AP(w_out, q4 * 512 * D, [[D, 128], [128 * D, 4], [1, D]])
            A("pool", lambda e, q4=q4, src=src: e.dma_start(out=wsb[:, q4 * 4:(q4 + 1) * 4, :], in_=src),
              writes=[("wsb", q4)], dma_sem=d_wo, after=ret_done)

        for half in range(2):
            if half == 1:
                load_wblock(0, UCOL + half * 512, 512, d_w[0])
                load_wblock(1, GCOL + half * 512, 512, d_w[1])
            if half == 1:
                for t in range(NT):
                    u_tile(t)
            for tb in range(NTB):
                for cc in range(4):
                    g = 2 * half + cc // 2
                    pbk = (tb * 4 + cc) % 4
                    def fm(e, tb=tb, cc=cc, g=g, pbk=pbk):
                        for tt in range(4):
                            t = tb * 4 + tt
                            o_ = bkf(pbk)[:, tt * 128:(tt + 1) * 128]
                            ins = e.matmul(o_, u_t[t][:, cc * 128:(cc + 1) * 128], Amat(g, 0 if t == 0 else 1),
                                           start=True, stop=(t == 0))
                            if t > 0:
                                ins = e.matmul(o_, u_t[t - 1][:, cc * 128:(cc + 1) * 128], Amat(g, 2),
                                               start=False, stop=True)
                        return ins
                    rd = [("u", t) for t in range(max(0, tb * 4 - 1), tb * 4 + 4)] + ["cbA"]
                    A("pe", fm, reads=rd, writes=[("ps", pbk)])
                    evac_copy(mixT[:, cc, tb * 512:(tb + 1) * 512], bkf(pbk), [("ps", pbk)], [("mix", cc, tb)], after=ret_done)
            for f in range(4):
                g = 2 * half + f // 2
                dd = f % 2
                for tb in range(NTB):
                    cs = slice(tb * 512, (tb + 1) * 512)
                    n_ = (f * 4 + tb)
                    pg = 4 + n_ % 2
                    pp = 6 + n_ % 2
                    def fg(e, f=f, cs=cs, pg=pg):
                        for kc in range(8):
                            ins = e.matmul(bkf(pg), wbuf[1][:, kc, f * 128:(f + 1) * 128], hT[:, kc, cs],
                                           start=(kc == 0), stop=(kc == 7))
                        return ins
                    A("pe", fg, reads=[("wbuf", 1)] + [("hT", t) for t in range(tb * 4, tb * 4 + 4)],
                      writes=[("ps", pg)])
                    sg = sgp[n_ % 3]
                    A("act", lambda e, sg=sg, pg=pg: e.activation(sg, bkf(pg), AF.Silu),
                      reads=[("ps", pg)], writes=[("sgp", n_ % 3)])
                    def fp(e, g=g, dd=dd, f=f, cs=cs, pp=pp):
                        for c2 in range(2):
                            ins = e.matmul(bkf(pp), pwv(g, c2, dd), mixT[:, 2 * (f // 2) + c2, cs],
                                           start=(c2 == 0), stop=(c2 == 1))
                        return ins
                    A("pe", fp, reads=[("pw", g // 2), ("mix", 2 * (f // 2), tb), ("mix", 2 * (f // 2) + 1, tb)],
                      writes=[("ps", pp)])
                    ych = 8 + half * 4 + f
                    A("dve", lambda e, ych=ych, cs=cs, pp=pp, sg=sg: e.scalar_tensor_tensor(
                        yT[:, ych, cs], bkf(pp), pscale(ych - 8), sg, ALU.mult, ALU.mult),
                      reads=[("ps", pp), "vf", ("sgp", n_ % 3)], writes=[("yT", ych, tb)], after=marks["pb"])

        Sc.barrier()

        A("sp", lambda e: e.dma_start(out=gpb, in_=bass.AP(gpost, 0, [[0, 128], [1, D]])), writes=["gpb"], dma_sem=d_c[4])
        for t in range(NT):
            xs_ = xst[t % 3]
            A("sp", lambda e, t=t, xs_=xs_: e.dma_start(out=xs_[:, :], in_=x[t * 128:(t + 1) * 128, :]),
              writes=[("xst", t % 3)], dma_sem=d_x[t % 3])
            b0 = (t % 2) * 2
            def fo(e, t=t, b0=b0):
                for n in range(2):
                    for fch in range(16):
                        ins = e.matmul(bkf(b0 + n), yT[:, fch, t * 128:(t + 1) * 128], wsb[:, fch, n * 512:(n + 1) * 512],
                                       start=(fch == 0), stop=(fch == 15))
                return ins
            A("pe", fo, reads=[("wsb", q4) for q4 in range(4)] + [("yT", c, t // 4) for c in range(16)],
              writes=[("ps", b0), ("ps", b0 + 1)])
            r2 = t % 2
            for n in range(2):
                A("act", lambda e, n=n, b0=b0, r2=r2: e.activation(ftmp[r2][:, n * 512:(n + 1) * 512], bkf(b0 + n), AF.Square,
                                                                   accum_out=fss[r2][:, n:n + 1]),
                  reads=[("ps", b0 + n)], writes=[("ftmp", r2, n), ("fss", r2, n)])
            A("dve", lambda e, r2=r2: e.tensor_tensor(fss[r2][:, 2:3], fss[r2][:, 0:1], fss[r2][:, 1:2], ALU.add),
              reads=[("fss", r2, 0), ("fss", r2, 1)], writes=[("fss", r2, 2)])
            A("dve", lambda e, r2=r2: e.tensor_scalar(fss[r2][:, 3:4], fss[r2][:, 2:3], 1.0 / D, EPS, ALU.mult, ALU.add),
              reads=[("fss", r2, 2)], writes=[("fss", r2, 3)])
            A("pool", lambda e, r2=r2: e.tensor_tensor(frs[r2][:, 0:1], fss[r2][:, 3:4], mhalf[:, 0:1], ALU.pow),
              reads=[("fss", r2, 3), "mhalf"], writes=[("frs", r2)])
            for n in range(2):
                hs = slice(n * 512, (n + 1) * 512)
                A("dve", lambda e, n=n, b0=b0, r2=r2, hs=hs: e.scalar_tensor_tensor(
                    ftmp[r2][:, hs], bkf(b0 + n), frs[r2][:, 0:1], gpb[:, hs],
                    ALU.mult, ALU.mult),
                  reads=[("ps", b0 + n), ("frs", r2), "gpb"], writes=[("ftmp", r2, n)])
                r3 = t % 3
                A("dve", lambda e, r2=r2, r3=r3, xs_=xs_, hs=hs: e.tensor_tensor(fout[r3][:, hs], ftmp[r2][:, hs], xs_[:, hs], ALU.add),
                  reads=[("ftmp", r2, n), ("xst", t % 3)], writes=[("fout", r3, n)])
                A("sp", lambda e, t=t, r3=r3, hs=hs: e.dma_start(out=out[t * 128:(t + 1) * 128, hs], in_=fout[r3][:, hs]),
                  reads=[("fout", r3, n)], dma_sem=d_o[r3][n])

        Sc.emit(sems, final_waits=[(d_o[a][b], 16 * len([t for t in range(NT) if t % 3 == a])) for a in range(3) for b in range(2)])
    return nc


def _consts():
    gam = 1.0 - 2.0 ** (-5.0 - np.arange(H, dtype=np.float64))
    s = np.arange(128, dtype=np.float64)
    cf = np.zeros((128, 24), np.float64)
    cf[:, 0:8] = gam[None, :] ** (-(s[:, None] + 1.0)) * (128.0 ** -0.5)
    cf[:, 8:16] = EPS * gam[None, :] ** (-2.0 * (s[:, None] + 1.0))
    half = 64
    inv_freq = 10000.0 ** (-np.arange(half, dtype=np.float64) / half)
    f_hi = inv_freq.astype(np.float32).astype(np.float64)
    f_lo = inv_freq - f_hi
    cf[:, 16] = np.concatenate([f_hi, f_hi])
    cf[:, 17] = np.concatenate([f_lo, f_lo])
    cb = np.zeros((128, 15 * 128), np.float64)
    cb[:, 0:128] = np.eye(128)
    P = np.zeros((128, 128))
    for m in range(128):
        P[(m + 64) % 128, m] = 1.0
    cb[:, 128:256] = P
    ss, cc = np.meshgrid(np.arange(128), np.arange(128), indexing="ij")
    cb[:, 256:384] = (cc >= ss).astype(np.float64)
    for g, w in enumerate(WINDOWS):
        t = cc; s_ = ss
        cnt = np.minimum(t + 1, w).astype(np.float64)
        a0 = ((s_ <= t) & (s_ > t - w)) / cnt - (s_ == t)
        a1 = ((s_ <= t) & (s_ > t - w)) / float(w) - (s_ == t)
        a2 = ((s_ - 128) > (t - w)) / float(w)
        cb[:, (3 + 3 * g + 0) * 128:(3 + 3 * g + 1) * 128] = a0
        cb[:, (3 + 3 * g + 1) * 128:(3 + 3 * g + 2) * 128] = a1
        cb[:, (3 + 3 * g + 2) * 128:(3 + 3 * g + 3) * 128] = a2
    return cf.astype(np.float32), cb.astype(np.float32)


_PROGRAM = None


def kernel(x, positions, w_in, w_out, pool_w, pool_scale, ret_norm_g, pre_norm_g, post_norm_g):
    global _PROGRAM
    x = np.asarray(x); positions = np.asarray(positions)
    w_in = np.asarray(w_in)[0]; w_out = np.asarray(w_out)[0]; pool_w = np.asarray(pool_w)[0]
    B = x.shape[0]
    assert B == 8 and x.shape[1] == S and x.shape[2] == D
    blocks = []
    for h in range(H):
        for p in (0, 1, 2, 3):
            blocks.append(w_in[:, p * D + h * 128: p * D + (h + 1) * 128])
    blocks.append(w_in[:, 4 * D:5 * D])
    blocks.append(w_in[:, 5 * D:6 * D])
    w_in_r = np.ascontiguousarray(np.concatenate(blocks, axis=1), dtype=np.float32)
    vecs = np.zeros((128, 24), np.float32)
    vecs[:, 0:8] = np.asarray(pre_norm_g)[0].reshape(8, 128).T
    vecs[:, 8:16] = np.asarray(ret_norm_g)[0].reshape(8, 128).T
    vecs[:, 16:24] = np.asarray(pool_scale)[0].reshape(8, 128).T
    gpost = np.ascontiguousarray(np.asarray(post_norm_g)[0].reshape(1, D), dtype=np.float32)
    cf, cb = _consts()
    if _PROGRAM is None:
        _PROGRAM = build_program()
    nc = _PROGRAM
    in_maps = []
    for b in range(B):
        in_maps.append({
            "x": np.ascontiguousarray(x[b], dtype=np.float32),
            "pos": np.ascontiguousarray(positions[b].reshape(1, S), dtype=np.int32),
            "w_in": w_in_r,
            "w_out": np.ascontiguousarray(w_out, dtype=np.float32),
            "pool_w": np.ascontiguousarray(pool_w.reshape(4 * 256, 256), dtype=np.float32),
            "vecs": vecs, "gpost": gpost, "constf": cf, "constb": cb,
        })
    res = run_bass_kernel_spmd(nc, in_maps, core_ids=list(range(B)))
    return np.stack([np.asarray(r["out"]).reshape(S, D) for r in res.results], axis=0).astype(np.float32)


if __name__ == "__main__":
    import time
    t0 = time.time()
    nc = build_program()
    print("built in", time.time() - t0)
```
